# Optimizing a Trainium2 kernel written in Bass

```python
import math
import jax, jax.numpy as jnp
from jax import lax
import numpy as np

D_MODEL = 1024
BATCH = 4
SEQ = 4096
DEPTH = 2

ROPE_THETA = 10000.0
Q_BLOCK = 128
EPS = 1e-6
NEG_INF = -1e30
D_FF = 2816
N_MOD = 9
DIFF_HEADS = 8
DIFF_HEAD_DIM = 64
MLA_HEADS = 8
MLA_NOPE_DIM = 64
MLA_ROPE_DIM = 32
MLA_V_DIM = 128
MLA_Q_RANK = 384
MLA_KV_RANK = 256
FOX_HEADS = 16
FOX_HEAD_DIM = 64

DIFF_QK_W = DIFF_HEADS * 2 * DIFF_HEAD_DIM
DIFF_V_W = DIFF_HEADS * 2 * DIFF_HEAD_DIM
MLA_QB_W = MLA_HEADS * (MLA_NOPE_DIM + MLA_ROPE_DIM)
MLA_KVB_W = MLA_HEADS * (MLA_NOPE_DIM + MLA_V_DIM)
EVEN_IN_W = 2 * DIFF_QK_W + DIFF_V_W + MLA_Q_RANK + MLA_KV_RANK + MLA_ROPE_DIM
EVEN_OUT_W = DIFF_V_W + MLA_HEADS * MLA_V_DIM
FOX_W = FOX_HEADS * FOX_HEAD_DIM
ODD_IN_W = 4 * FOX_W + FOX_HEADS

kernel_name = 'hybrid_diff_mla_fox_macaron_adaln'


def rms_norm(x, g=None):
    xf = x.astype(jnp.float32)
    y = xf * lax.rsqrt(jnp.mean(xf * xf, axis=-1, keepdims=True) + EPS)
    if g is not None:
        y = y * g.astype(jnp.float32)
    return y.astype(x.dtype)


def modulate(x, shift, scale):
    return rms_norm(x) * (1.0 + scale) + shift


def swiglu(h, w_gate, w_up, w_down):
    return (jax.nn.silu(h @ w_gate) * (h @ w_up)) @ w_down


def rope_tables(positions, dim):
    inv = ROPE_THETA ** (-jnp.arange(0, dim, 2, dtype=jnp.float32) / dim)
    ang = positions.astype(jnp.float32)[..., None] * inv
    return jnp.cos(ang)[:, None], jnp.sin(ang)[:, None]


def apply_rope(x, cos, sin):
    xf = x.astype(jnp.float32)
    x1, x2 = jnp.split(xf, 2, axis=-1)
    return jnp.concatenate([x1 * cos - x2 * sin, x2 * cos + x1 * sin], axis=-1).astype(x.dtype)


def to_heads(t, n):
    B, S, W = t.shape
    return t.reshape(B, S, n, W // n).transpose(0, 2, 1, 3)


def merge_heads(t):
    B, H, S, d = t.shape
    return t.transpose(0, 2, 1, 3).reshape(B, S, H * d)


def split_cols(t, widths):
    idx = [int(i) for i in np.cumsum(widths)[:-1]]
    return jnp.split(t, idx, axis=-1)


def causal_attention(q, k, v, scale, log_f_cum=None):
    B, H, S, dk = q.shape
    nb = S // Q_BLOCK
    q_blocks = q.reshape(B, H, nb, Q_BLOCK, dk).transpose(2, 0, 1, 3, 4)
    c_blocks = None if log_f_cum is None else log_f_cum.reshape(B, H, nb, Q_BLOCK).transpose(2, 0, 1, 3)
    k_pos = jnp.arange(S)

    def block(args):
        i, q_blk, c_blk = args
        q_pos = i * Q_BLOCK + jnp.arange(Q_BLOCK)
        logits = jnp.einsum('bhqd,bhkd->bhqk', q_blk, k).astype(jnp.float32) * scale
        if c_blk is not None:
            logits = logits + (c_blk[..., :, None] - log_f_cum[..., None, :])
        logits = jnp.where(k_pos[None, :] <= q_pos[:, None], logits, NEG_INF)
        p = jax.nn.softmax(logits, axis=-1)
        return jnp.einsum('bhqk,bhkd->bhqd', p.astype(v.dtype), v)

    out = lax.map(block, (jnp.arange(nb), q_blocks, c_blocks))
    return out.transpose(1, 2, 0, 3, 4).reshape(B, H, S, v.shape[-1])


def ab_mixer(h, cos_a, sin_a, cos_b, sin_b, lambda_init, w_in, w_qb, w_kvb, q_lat_g, kv_lat_g,
             diff_q_g, diff_k_g, mla_q_g, mla_k_g, lam_q1, lam_k1, lam_q2, lam_k2, subln_g, w_out):
    B, S, _ = h.shape
    q_a, k_a, v_a, c_q, c_kv, k_r = split_cols(
        h @ w_in, [DIFF_QK_W, DIFF_QK_W, DIFF_V_W, MLA_Q_RANK, MLA_KV_RANK, MLA_ROPE_DIM])

    q_a = q_a.reshape(B, S, DIFF_HEADS, 2, DIFF_HEAD_DIM)
    k_a = k_a.reshape(B, S, DIFF_HEADS, 2, DIFF_HEAD_DIM)

    def qk(t, j, g):
        return apply_rope(rms_norm(t[:, :, :, j].transpose(0, 2, 1, 3), g), cos_a, sin_a)

    q1, q2 = qk(q_a, 0, diff_q_g), qk(q_a, 1, diff_q_g)
    k1, k2 = qk(k_a, 0, diff_k_g), qk(k_a, 1, diff_k_g)
    v = to_heads(v_a, DIFF_HEADS)
    lam = (jnp.exp(jnp.sum(lam_q1.astype(jnp.float32) * lam_k1.astype(jnp.float32)))
           - jnp.exp(jnp.sum(lam_q2.astype(jnp.float32) * lam_k2.astype(jnp.float32)))
           + lambda_init)
    scale_a = DIFF_HEAD_DIM ** -0.5
    o = (causal_attention(q1, k1, v, scale_a).astype(jnp.float32)
         - lam * causal_attention(q2, k2, v, scale_a).astype(jnp.float32))
    o_a = (rms_norm(o, subln_g) * (1.0 - lambda_init)).astype(h.dtype)

    q_b = to_heads(rms_norm(c_q, q_lat_g) @ w_qb, MLA_HEADS)
    kv_b = to_heads(rms_norm(c_kv, kv_lat_g) @ w_kvb, MLA_HEADS)
    k_nope, v_b = kv_b[..., :MLA_NOPE_DIM], kv_b[..., MLA_NOPE_DIM:]
    k_rope = jnp.broadcast_to(k_r[:, None], (B, MLA_HEADS, S, MLA_ROPE_DIM))
    q_b = rms_norm(q_b, mla_q_g)
    k_b = rms_norm(jnp.concatenate([k_nope, k_rope], axis=-1), mla_k_g)
    q_b = jnp.concatenate([q_b[..., :MLA_NOPE_DIM], apply_rope(q_b[..., MLA_NOPE_DIM:], cos_b, sin_b)], axis=-1)
    k_b = jnp.concatenate([k_b[..., :MLA_NOPE_DIM], apply_rope(k_b[..., MLA_NOPE_DIM:], cos_b, sin_b)], axis=-1)
    o_b = causal_attention(q_b, k_b, v_b, (MLA_NOPE_DIM + MLA_ROPE_DIM) ** -0.5)

    return jnp.concatenate([merge_heads(o_a), merge_heads(o_b)], axis=-1) @ w_out


def fox_mixer(h, w_in, b_f, q_g, k_g, w_out):
    q, k, v, og, f_logit = split_cols(h @ w_in, [FOX_W, FOX_W, FOX_W, FOX_W, FOX_HEADS])
    q = rms_norm(to_heads(q, FOX_HEADS), q_g)
    k = rms_norm(to_heads(k, FOX_HEADS), k_g)
    v = to_heads(v, FOX_HEADS)
    log_f = jax.nn.log_sigmoid((f_logit + b_f).astype(jnp.float32))
    log_f_cum = jnp.cumsum(log_f.transpose(0, 2, 1), axis=-1)
    o = causal_attention(q, k, v, FOX_HEAD_DIM ** -0.5, log_f_cum)
    return (merge_heads(o) * jax.nn.sigmoid(og)) @ w_out


def setup_inputs(seed: int = 0) -> dict:
    key = jax.random.key(seed)
    keys = jax.random.split(key, 40)
    counter = [0]

    def nk():
        counter[0] += 1
        return keys[counter[0] - 1]

    def w(shape, fan_in, mult=1.0):
        return mult * fan_in ** -0.5 * jax.random.normal(nk(), shape, jnp.float32)

    def gain(shape):
        return 1.0 + 0.1 * jax.random.normal(nk(), shape, jnp.float32)

    n_even, n_odd = (DEPTH + 1) // 2, DEPTH // 2
    D, F = D_MODEL, D_FF
    x = jax.random.normal(nk(), (BATCH, SEQ, D), jnp.float32)
    c = jax.random.normal(nk(), (BATCH, D), jnp.float32)
    positions = (jax.random.randint(nk(), (BATCH, 1), 0, 1024, jnp.int32)
                 + jnp.arange(SEQ, dtype=jnp.int32)[None, :])
    return {
        'x': x,
        'c': c,
        'positions': positions,
        'ada_w': w((DEPTH, D, N_MOD * D), D, 0.5),
        'ada_b': 0.02 * jax.random.normal(nk(), (DEPTH, N_MOD * D), jnp.float32),
        'ff1_gate': w((DEPTH, D, F), D),
        'ff1_up': w((DEPTH, D, F), D),
        'ff1_down': w((DEPTH, F, D), F),
        'ff2_gate': w((DEPTH, D, F), D),
        'ff2_up': w((DEPTH, D, F), D),
        'ff2_down': w((DEPTH, F, D), F),
        'ab_w_in': w((n_even, D, EVEN_IN_W), D),
        'mla_w_qb': w((n_even, MLA_Q_RANK, MLA_QB_W), MLA_Q_RANK),
        'mla_w_kvb': w((n_even, MLA_KV_RANK, MLA_KVB_W), MLA_KV_RANK),
        'mla_q_lat_g': gain((n_even, MLA_Q_RANK)),
        'mla_kv_lat_g': gain((n_even, MLA_KV_RANK)),
        'diff_q_g': gain((n_even, DIFF_HEAD_DIM)),
        'diff_k_g': gain((n_even, DIFF_HEAD_DIM)),
        'mla_q_g': gain((n_even, MLA_NOPE_DIM + MLA_ROPE_DIM)),
        'mla_k_g': gain((n_even, MLA_NOPE_DIM + MLA_ROPE_DIM)),
        'diff_lam_q1': 0.1 * jax.random.normal(nk(), (n_even, DIFF_HEAD_DIM), jnp.float32),
        'diff_lam_k1': 0.1 * jax.random.normal(nk(), (n_even, DIFF_HEAD_DIM), jnp.float32),
        'diff_lam_q2': 0.1 * jax.random.normal(nk(), (n_even, DIFF_HEAD_DIM), jnp.float32),
        'diff_lam_k2': 0.1 * jax.random.normal(nk(), (n_even, DIFF_HEAD_DIM), jnp.float32),
        'diff_subln_g': gain((n_even, 2 * DIFF_HEAD_DIM)),
        'ab_w_out': w((n_even, EVEN_OUT_W, D), EVEN_OUT_W),
        'fox_w_in': w((n_odd, D, ODD_IN_W), D),
        'fox_b_f': jax.random.uniform(nk(), (n_odd, FOX_HEADS), jnp.float32, 1.0, 5.0),
        'fox_q_g': gain((n_odd, FOX_HEAD_DIM)),
        'fox_k_g': gain((n_odd, FOX_HEAD_DIM)),
        'fox_w_out': w((n_odd, FOX_W, D), FOX_W),
    }


def reference(x, c, positions, ada_w, ada_b, ff1_gate, ff1_up, ff1_down, ff2_gate, ff2_up, ff2_down,
              ab_w_in, mla_w_qb, mla_w_kvb, mla_q_lat_g, mla_kv_lat_g, diff_q_g, diff_k_g, mla_q_g, mla_k_g,
              diff_lam_q1, diff_lam_k1, diff_lam_q2, diff_lam_k2, diff_subln_g, ab_w_out,
              fox_w_in, fox_b_f, fox_q_g, fox_k_g, fox_w_out):
    cos_a, sin_a = rope_tables(positions, DIFF_HEAD_DIM)
    cos_b, sin_b = rope_tables(positions, MLA_ROPE_DIM)
    cond = jax.nn.silu(c)
    for l in range(DEPTH):
        mod = cond @ ada_w[l] + ada_b[l]
        sh1, sc1, g1, sh2, sc2, g2, sh3, sc3, g3 = [m[:, None, :] for m in jnp.split(mod, N_MOD, axis=-1)]
        x = x + 0.5 * g1 * swiglu(modulate(x, sh1, sc1), ff1_gate[l], ff1_up[l], ff1_down[l])
        h = modulate(x, sh2, sc2)
        if l % 2 == 0:
            e = l // 2
            lambda_init = 0.8 - 0.6 * math.exp(-0.3 * l)
            mix = ab_mixer(h, cos_a, sin_a, cos_b, sin_b, lambda_init, ab_w_in[e], mla_w_qb[e], mla_w_kvb[e],
                           mla_q_lat_g[e], mla_kv_lat_g[e], diff_q_g[e], diff_k_g[e], mla_q_g[e], mla_k_g[e],
                           diff_lam_q1[e], diff_lam_k1[e], diff_lam_q2[e], diff_lam_k2[e], diff_subln_g[e],
                           ab_w_out[e])
        else:
            o = l // 2
            mix = fox_mixer(h, fox_w_in[o], fox_b_f[o], fox_q_g[o], fox_k_g[o], fox_w_out[o])
        x = x + g2 * mix
        x = x + 0.5 * g3 * swiglu(modulate(x, sh3, sc3), ff2_gate[l], ff2_up[l], ff2_down[l])
    return x
```

```python
import math
import numpy as np
import ml_dtypes
import concourse.bass as bass
import concourse.mybir as mybir
from concourse.bass_utils import run_bass_kernel_spmd

F32 = mybir.dt.float32
BF16 = mybir.dt.bfloat16
I32 = mybir.dt.int32
AF = mybir.ActivationFunctionType
ALU = mybir.AluOpType
AX = mybir.AxisListType

D = 1024
DFF = 2816
NFC = DFF // 128
EPS = 1e-6
NEG = -30000.0
N_CORES = 8
ENGS = ["pe", "act", "dve", "pool", "sp"]
NS = 8


class Buf:
    __slots__ = ("name", "w", "r", "excl")

    def __init__(self, name, excl=False):
        self.name = name
        self.w = None
        self.r = []
        self.excl = excl


class Op:
    __slots__ = ("eng", "idx", "fn", "deps", "signal", "kind", "semi", "val", "phase")


class Sched:
    def __init__(self, nc):
        self.nc = nc
        self.ops = {e: [] for e in ENGS}
        self.dmas = {e: [] for e in ENGS}
        self.seen = {e: {} for e in ENGS}
        self.seen_dma = {e: set() for e in ENGS}
        self.ccs = []
        self.out_ops = []
        self.pending = {e: [] for e in ENGS}
        self.phase = "prologue"

    def barrier(self):
        deps = []
        for e in ENGS:
            comp = [o for o in self.ops[e] if o.kind == "c"]
            if comp:
                deps.append(comp[-1])
            deps.extend(self.dmas[e][-NS:])
        deps.extend(self.ccs[-NS:])
        for e in ENGS:
            self.pending[e] = list(deps)

    def _add(self, eng, fn, reads, writes, kind):
        op = Op()
        op.eng, op.fn, op.kind, op.signal, op.semi, op.val = eng, fn, kind, False, None, None
        op.phase = self.phase
        deps = []
        raw = set()
        for b in reads:
            if b.w is not None:
                deps.append(b.w)
                raw.add(id(b.w))
            if b.excl:
                deps.extend(r for r in b.r if r.eng != eng)
        for b in writes:
            if b.w is not None:
                deps.append(b.w)
            deps.extend(b.r)
        if self.pending[eng]:
            for d in self.pending[eng]:
                deps.append(d)
                raw.add(id(d))
            self.pending[eng] = []
        if kind == "d":
            k = len(self.dmas[eng])
            if k >= NS:
                deps.append(self.dmas[eng][k - NS])
        if kind == "cc":
            k = len(self.ccs)
            if k >= NS:
                deps.append(self.ccs[k - NS])
        final = []
        for d in deps:
            if d.kind == "c":
                if d.eng == eng and kind == "c":
                    if eng == "pe" or id(d) not in raw:
                        continue
                if self.seen[eng].get(d.eng, -1) >= d.idx:
                    continue
                self.seen[eng][d.eng] = d.idx
                final.append(d)
            else:
                if id(d) in self.seen_dma[eng]:
                    continue
                self.seen_dma[eng].add(id(d))
                final.append(d)
        op.deps = final
        op.idx = len(self.ops[eng])
        self.ops[eng].append(op)
        if kind == "d":
            k = len(self.dmas[eng])
            op.semi, op.val = k % NS, 16 * (k // NS + 1)
            self.dmas[eng].append(op)
        if kind == "cc":
            k = len(self.ccs)
            op.semi, op.val = k % NS, k // NS + 1
            self.ccs.append(op)
        for b in reads:
            if kind == "c":
                b.r = [o for o in b.r if not (o.kind == "c" and o.eng == eng)]
            b.r.append(op)
        for b in writes:
            b.w = op
            b.r = []
        return op

    def op(self, eng, fn, reads=(), writes=()):
        return self._add(eng, fn, list(reads), list(writes), "c")

    def dma(self, eng, out, in_, reads=(), writes=(), is_out=False):
        o = self._add(eng, lambda e: e.dma_start(out=out, in_=in_), list(reads), list(writes), "d")
        if is_out:
            self.out_ops.append(o)
        return o

    def cc(self, fn, reads=(), writes=()):
        return self._add("pool", fn, list(reads), list(writes), "cc")

    def emit(self):
        nc = self.nc
        for e in ENGS:
            for op in self.ops[e]:
                for d in op.deps:
                    if d.kind == "c":
                        d.signal = True
        for e in ENGS:
            n = 0
            for op in self.ops[e]:
                if op.kind == "c" and op.signal:
                    n += 1
                    op.val = n
        sems = {e: nc.alloc_semaphore("s_" + e) if hasattr(nc, "alloc_semaphore") else None for e in ENGS}
        return sems


def build_program(NT=16, stop=None, lite=(), skip=()):
    nc = bass.Bass("TRN2", target_bir_lowering=False)
    K = Sched(nc)
    S_OWN = NT * 128
    S_ALL = 2 * S_OWN
    CH = min(8, NT)
    NCHK = NT // CH
    NG = NT // 4
    CW = CH * 128

    def din(name, shape, dt=F32):
        if name in lite:
            shape = [1] * len(shape)
        return nc.dram_tensor(name, list(shape), dt, kind="ExternalInput").ap()

    x_d = din("x", [S_OWN, D])
    cT_d = din("cT", [128, 8])
    pos_d = din("pos", [128, NT], I32)
    ada_w_d = din("ada_w", [2, D, 9 * D])
    ada_b_d = din("ada_b", [2, 9 * D])
    ffw = {}
    for nm in ("ff1_gate", "ff1_up", "ff2_gate", "ff2_up"):
        ffw[nm] = din(nm, [2, D, DFF])
    for nm in ("ff1_down", "ff2_down"):
        ffw[nm] = din(nm, [2, DFF, D])
    ab_w_in_d = din("ab_w_in", [1, D, 3744])
    w_qb_d = din("mla_w_qb", [1, 384, 768])
    w_kvb_d = din("mla_w_kvb", [1, 256, 1536])
    q_lat_g_d = din("mla_q_lat_g", [1, 384])
    kv_lat_g_d = din("mla_kv_lat_g", [1, 256])
    diff_q_g_d = din("diff_q_g", [1, 64])
    diff_k_g_d = din("diff_k_g", [1, 64])
    mla_q_g_d = din("mla_q_g", [1, 96])
    mla_k_g_d = din("mla_k_g", [1, 96])
    lam_d = [din(n, [1, 64]) for n in ("diff_lam_q1", "diff_lam_k1", "diff_lam_q2", "diff_lam_k2")]
    subln_g_d = din("diff_subln_g", [1, 128])
    ab_w_out_d = din("ab_w_out", [1, 2048, D])
    fox_w_in_d = din("fox_w_in", [1, D, 4112])
    fox_b_f_d = din("fox_b_f", [1, 16])
    fox_q_g_d = din("fox_q_g", [1, 64])
    fox_k_g_d = din("fox_k_g", [1, 64])
    fox_w_out_d = din("fox_w_out", [1, D, D])
    ident_bf_d = din("ident_bf", [128, 128], BF16)
    ident_f_d = din("ident_f", [128, 128])
    masks_d = din("masks", [128, 4, 128], BF16)
    inv_a_d = din("inv_a", [1, 32])
    inv_b_d = din("inv_b", [1, 16])
    sel_d = din("sel", [1, 2])
    y_d = nc.dram_tensor("y", [S_OWN, D], F32, kind="ExternalOutput").ap()

    def dscr(name, shape, dt=BF16):
        return nc.dram_tensor(name, list(shape), dt)

    qtA_s = dscr("qtA", [8, 128, S_OWN])
    qtB_s = dscr("qtB", [8, 96, S_OWN])
    ktA_loc = [dscr(f"ktA_loc{g}", [4 * 128, S_OWN]) for g in range(2)]
    ktA_all = [dscr(f"ktA_all{g}", [2 * 4 * 128, S_OWN]) for g in range(2)]
    ktB_loc = [dscr(f"ktB_loc{g}", [4 * 96, S_OWN]) for g in range(2)]
    ktB_all = [dscr(f"ktB_all{g}", [2 * 4 * 96, S_OWN]) for g in range(2)]
    NVH = 2 if NT >= 2 else 1
    VR = S_OWN // NVH
    vA_loc = [dscr(f"vA_loc{g}", [VR, 1024]) for g in range(NVH)]
    vA_all = [dscr(f"vA_all{g}", [2 * VR, 1024]) for g in range(NVH)]
    vB_loc = [dscr(f"vB_loc{g}", [VR, 1024]) for g in range(NVH)]
    vB_all = [dscr(f"vB_all{g}", [2 * VR, 1024]) for g in range(NVH)]
    attn0_s = dscr("attn0", [16, 128, S_OWN])
    qtF_s = dscr("qtF", [16, 70, S_OWN])
    ktF_loc = [dscr(f"ktF_loc{g}", [4 * 70, S_OWN]) for g in range(4)]
    ktF_all = [dscr(f"ktF_all{g}", [2 * 4 * 70, S_OWN]) for g in range(4)]
    vF_loc = [dscr(f"vF_loc{g}", [VR, 1024]) for g in range(NVH)]
    vF_all = [dscr(f"vF_all{g}", [2 * VR, 1024]) for g in range(NVH)]
    lf_loc = dscr("lf_loc", [16, S_OWN], F32)
    lf_all = dscr("lf_all", [32, S_OWN], F32)
    attn1_s = dscr("attn1", [8, 128, S_OWN])
    og_s = dscr("og", [S_OWN, 1024])

    def sb(name, shape, dt=F32):
        return nc.alloc_sbuf_tensor(name, list(shape), dt)

    x_sb = sb("x_sb", [128, NT, D])
    xB = [Buf(f"x{t}") for t in range(NT)]
    ident_bf = sb("ident_bf_sb", [128, 128], BF16)
    ident_f = sb("ident_f_sb", [128, 128])
    masks = sb("masks_sb", [128, 4, 128], BF16)
    cB = Buf("consts")
    cosA = sb("cosA", [128, NT, 32]); sinA = sb("sinA", [128, NT, 32])
    cosB = sb("cosB", [128, NT, 16]); sinB = sb("sinB", [128, NT, 16])
    ropeB = Buf("rope")
    neghalf = sb("neghalf", [128, 16])
    cond = sb("cond", [128, 8])
    condB = Buf("cond")
    gate = sb("gate", [128, D])
    gateB = Buf("gate")
    modcol = sb("modcol", [128, 2, 8])
    modcolB = Buf("modcol")
    adab_row = sb("adab_row", [16, 128])
    adab_rowB = Buf("adab_row")
    xs_bf = sb("xs_bf", [128, D], BF16)
    xsB = Buf("xs")
    junk = sb("junk", [128, D], BF16)
    junkB = Buf("junk")
    st = sb("stats", [128, 256])
    stB = [Buf(f"st{i}") for i in range(256)]
    ARENA = 124 * 1024
    arena = sb("arena", [128, ARENA // 2], BF16)

    class Bump:
        def __init__(self):
            self.off = 0
            K.barrier()

        def __call__(self, shape, dt):
            esz = 2 if dt == BF16 else 4
            n = int(np.prod(shape[1:])) * esz
            n = (n + 63) // 64 * 64
            v = aview(self.off, shape, dt)
            self.off += n
            assert self.off <= ARENA, (self.off, ARENA)
            return v

    def aview(off, shape, dt):
        esz = 2 if dt == BF16 else 4
        n = int(np.prod(shape[1:]))
        v = arena[0:shape[0], off // 2: off // 2 + n * esz // 2]
        if dt != BF16:
            v = v.bitcast(dt)
        if len(shape) == 3:
            v = v.rearrange("p (a b) -> p a b", a=shape[1])
        return v

    ps = [nc.alloc_psum_tensor(f"ps{i}", [128, 512], F32) for i in range(8)]
    psB = [Buf(f"ps{i}", excl=True) for i in range(8)]

    def ps_bf(i):
        return ps[i][:, :].bitcast(BF16)

    sched = K
    V, A, P_, T, SP = "dve", "act", "pool", "pe", "sp"

    K.dma(SP, ident_bf[:, :], ident_bf_d, writes=[cB])
    K.dma(SP, ident_f[:, :], ident_f_d, writes=[cB])
    K.dma(SP, masks[:, :, :], masks_d, writes=[cB])
    for t in range(NT):
        K.dma(SP, x_sb[:, t, :], x_d[t * 128:(t + 1) * 128, :], writes=[xB[t]])
    K.dma(SP, cond[:, :], cT_d, writes=[condB])
    K.op(P_, lambda e: e.memset(neghalf[:, :], -0.5), writes=[cB])
    K.op(A, lambda e: e.activation(out=cond[:, :], in_=cond[:, :], func=AF.Silu), reads=[condB], writes=[condB])

    pos_i = sb("pos_i", [128, NT], I32)
    pos_f = sb("pos_f", [128, NT])
    inv_a = sb("inv_a_sb", [128, 32]); inv_b = sb("inv_b_sb", [128, 16])
    K.dma(SP, pos_i[:, :], pos_d, writes=[ropeB])
    K.dma(SP, inv_a[:, :], inv_a_d.partition_broadcast(128), writes=[ropeB])
    K.dma(SP, inv_b[:, :], inv_b_d.partition_broadcast(128), writes=[ropeB])
    K.op(V, lambda e: e.tensor_copy(out=pos_f[:, :], in_=pos_i[:, :]), reads=[ropeB], writes=[ropeB])
    TWO_PI = 2.0 * math.pi
    ri = aview(0, [128, NT, 32], F32).bitcast(I32)
    rf = aview(NT * 32 * 4, [128, NT, 32], F32)
    for (ct, stt, inv, nf) in ((cosA, sinA, inv_a, 32), (cosB, sinB, inv_b, 16)):
        K.op(V, lambda e, inv=inv: e.tensor_scalar(out=inv[:, :], in0=inv[:, :], scalar1=1.0 / TWO_PI, scalar2=None,
                                                   op0=ALU.mult), reads=[ropeB], writes=[ropeB])
        for (dst, shift) in ((stt, 0.5), (ct, 0.75)):
            for t in range(NT):
                K.op(V, lambda e, dst=dst, t=t, inv=inv, shift=shift: e.tensor_scalar(
                    out=dst[:, t, :], in0=inv[:, :], scalar1=pos_f[:, t:t + 1], scalar2=shift,
                    op0=ALU.mult, op1=ALU.add), reads=[ropeB], writes=[ropeB])
            K.op(V, lambda e, dst=dst, nf=nf: e.tensor_copy(out=ri[:, :, 0:nf], in_=dst[:, :, :]), reads=[ropeB], writes=[ropeB])
            K.op(V, lambda e, dst=dst, nf=nf: e.tensor_copy(out=rf[:, :, 0:nf], in_=ri[:, :, 0:nf]), reads=[ropeB], writes=[ropeB])
            K.op(V, lambda e, dst=dst, nf=nf: e.tensor_tensor(out=dst[:, :, :], in0=dst[:, :, :], in1=rf[:, :, 0:nf],
                                                              op=ALU.subtract), reads=[ropeB], writes=[ropeB])
            K.op(V, lambda e, dst=dst, nf=nf: e.tensor_scalar(out=rf[:, :, 0:nf], in0=dst[:, :, :], scalar1=0.0, scalar2=None,
                                                              op0=ALU.is_lt), reads=[ropeB], writes=[ropeB])
            K.op(V, lambda e, dst=dst, nf=nf: e.tensor_tensor(out=dst[:, :, :], in0=dst[:, :, :], in1=rf[:, :, 0:nf],
                                                              op=ALU.add), reads=[ropeB], writes=[ropeB])
            K.op(V, lambda e, dst=dst: e.tensor_scalar(out=dst[:, :, :], in0=dst[:, :, :], scalar1=TWO_PI, scalar2=-math.pi,
                                                       op0=ALU.mult, op1=ALU.add), reads=[ropeB], writes=[ropeB])
            K.op(V, lambda e, dst=dst: e.tensor_scalar(out=dst[:, :, :], in0=dst[:, :, :], scalar1=math.pi, scalar2=-math.pi,
                                                       op0=ALU.min, op1=ALU.max), reads=[ropeB], writes=[ropeB])
            K.op(A, lambda e, dst=dst: e.activation(out=dst[:, :, :], in_=dst[:, :, :], func=AF.Sin),
                 reads=[ropeB], writes=[ropeB])

    def compute_mod(l, sub, gate_mult):
        prev_phase = K.phase
        K.phase = f"mod_l{l}s{sub}"
        al = Bump()
        aw = [al([128, 8, 512], F32) for i in range(2)]
        awB = [Buf(f"aw{i}") for i in range(2)]
        cond_rep = al([128, 8, 128], F32)
        crB = Buf("cond_rep")
        K.op(V, lambda e: e.tensor_copy(out=cond_rep, in_=cond[:, :].unsqueeze(2).broadcast_to([128, 8, 128])),
             reads=[condB], writes=[crB])
        m_sh, m_sc, m_g = 3 * sub, 3 * sub + 1, 3 * sub + 2
        K.dma(SP, adab_row[:, :], ada_b_d[l, m_sh * D:(m_sh + 2) * D].rearrange("(r c) -> r c", c=128),
              writes=[adab_rowB])
        gb = al([128, D], F32)
        gbB = Buf("gb")
        K.dma(SP, gb, ada_b_d[l:l + 1, m_g * D:(m_g + 1) * D].partition_broadcast(128), writes=[gbB])
        bi = 0
        colps = ps[7]
        first_col = True
        for (m, kind) in ((m_sh, "col"), (m_sc, "col"), (m_g, "row")):
            for half in range(2):
                a = aw[bi % 2]; aB = awB[bi % 2]; bi += 1
                c0 = m * D + half * 512
                K.dma(SP, a, ada_w_d[l].rearrange("(k p) c -> p k c", p=128)[:, :, c0:c0 + 512], writes=[aB])
                if kind == "row":
                    pst = ps[half]
                    for k in range(8):
                        K.op(T, lambda e, a=a, k=k, pst=pst: e.matmul(pst[:, :], lhsT=cond_rep[:, k, :], rhs=a[:, k, :],
                                                                      start=(k == 0), stop=(k == 7)),
                             reads=[aB, crB], writes=[psB[half]])
                    K.op(V, lambda e, pst=pst, half=half: e.tensor_tensor(
                        out=gate[:, half * 512:(half + 1) * 512], in0=pst[:, :], in1=gb[:, half * 512:(half + 1) * 512],
                        op=ALU.add), reads=[psB[half], gbB], writes=[gateB])
                else:
                    which = 0 if m == m_sc else 1
                    for jb in range(4):
                        col = which * 8 + half * 4 + jb
                        for k in range(8):
                            K.op(T, lambda e, a=a, k=k, jb=jb, col=col: e.matmul(
                                colps[:, col:col + 1], lhsT=a[:, k, jb * 128:(jb + 1) * 128], rhs=cond[:, k:k + 1],
                                start=(k == 0), stop=(k == 7)), reads=[aB, condB], writes=[psB[7]])
        if gate_mult != 1.0:
            K.op(V, lambda e: e.tensor_scalar(out=gate[:, :], in0=gate[:, :], scalar1=gate_mult, scalar2=None,
                                              op0=ALU.mult), reads=[gateB], writes=[gateB])
        K.op(T, lambda e: e.transpose(ps[6][:, 0:16], adab_row[:, :], ident_f[0:16, 0:16]),
             reads=[adab_rowB, cB], writes=[psB[6]])
        K.op(V, lambda e: e.tensor_copy(out=st[:, 0:16], in_=ps[6][:, 0:16]), reads=[psB[6]], writes=[stB[0]])
        K.op(V, lambda e: e.scalar_tensor_tensor(out=modcol[:, 0, :], in0=colps[:, 0:8], scalar=1.0, in1=st[:, 8:16],
                                                 op0=ALU.add, op1=ALU.add), reads=[psB[7], stB[0]], writes=[modcolB])
        K.op(V, lambda e: e.tensor_tensor(out=modcol[:, 1, :], in0=colps[:, 8:16], in1=st[:, 0:8], op=ALU.add),
             reads=[psB[7], stB[0]], writes=[modcolB])
        K.phase = prev_phase

    trps = [6, 7]
    trcnt = [0]

    def mod_transpose(t, dstT, col0, dstB):
        xt = x_sb[:, t, :]
        K.op(V, lambda e: e.scalar_tensor_tensor(out=junk[:, :], in0=xt, scalar=1.0, in1=xt, op0=ALU.mult, op1=ALU.mult,
                                                 accum_out=st[:, 16:17]), reads=[xB[t]], writes=[junkB, stB[16]])
        K.op(V, lambda e: e.tensor_scalar(out=st[:, 17:18], in0=st[:, 16:17], scalar1=1.0 / D, scalar2=EPS,
                                          op0=ALU.mult, op1=ALU.add), reads=[stB[16]], writes=[stB[17]])
        K.op(P_, lambda e: e.tensor_tensor(out=st[:, 18:19], in0=st[:, 17:18], in1=neghalf[:, 0:1], op=ALU.pow),
             reads=[stB[17], cB], writes=[stB[18]])
        K.op(V, lambda e: e.tensor_scalar(out=xs_bf[:, :], in0=xt, scalar1=st[:, 18:19], scalar2=None, op0=ALU.mult),
             reads=[xB[t], stB[18]], writes=[xsB])
        pi = trps[trcnt[0] % 2]; trcnt[0] += 1
        pv = ps_bf(pi)
        for k in range(8):
            K.op(T, lambda e, k=k: e.transpose(pv[:, k * 128:(k + 1) * 128], xs_bf[:, k * 128:(k + 1) * 128], ident_bf[:, :]),
                 reads=[xsB, cB], writes=[psB[pi]])
        for k in range(8):
            if k % 2 == 1:
                K.op(A, lambda e, k=k: e.activation(out=dstT[:, k, col0:col0 + 128], in_=pv[:, k * 128:(k + 1) * 128],
                                                    func=AF.Identity, scale=modcol[:, 0, k:k + 1], bias=modcol[:, 1, k:k + 1]),
                     reads=[psB[pi], modcolB], writes=[dstB])
                continue
            K.op(V, lambda e, k=k: e.tensor_scalar(out=dstT[:, k, col0:col0 + 128], in0=pv[:, k * 128:(k + 1) * 128],
                                                   scalar1=modcol[:, 0, k:k + 1], scalar2=modcol[:, 1, k:k + 1],
                                                   op0=ALU.mult, op1=ALU.add),
                 reads=[psB[pi], modcolB], writes=[dstB])

    def ffn(l, sub, wg_d, wu_d, wd_d):
        K.phase = f"ffn_l{l}s{sub}"
        compute_mod(l, sub, 0.5)
        al = Bump()
        xnT = al([128, 8, CW], BF16)
        hT = al([128, NFC, CW], BF16)
        wd = [al([128, NFC, 512], BF16) for i in range(2)]
        wgu = [[al([128, 8, 256], BF16) for j in range(2)] for i in range(2)]
        tmp = [al([128, 512], F32) for i in range(2)]
        xnB = [Buf(f"xnT{c}") for c in range(CH)]
        hB = [[Buf(f"hT{fc}_{tg}") for tg in range(CW // 512 if CW >= 512 else 1)] for fc in range(NFC)]
        wdB = [Buf("wd0"), Buf("wd1")]
        wguB = [Buf("wgu0"), Buf("wgu1")]
        tmpB = [Buf("tmp0"), Buf("tmp1")]
        TG = max(1, CW // 512)
        TW = min(512, CW)
        for ck in range(NCHK):
            for c in range(CH):
                mod_transpose(ck * CH + c, xnT, c * 128, xnB[c])
            cnt = 0
            for j in range(NFC // 2):
                wb = j % 2
                if j in (2, 4):
                    dh_ = (j - 2) // 2
                    K.dma(P_, wd[dh_], wd_d[l].rearrange("(k p) d -> p k d", p=128)[:, :, dh_ * 512:(dh_ + 1) * 512],
                          writes=[wdB[dh_]])
                K.dma(P_, wgu[wb][0], wg_d[l].rearrange("(k p) f -> p k f", p=128)[:, :, j * 256:(j + 1) * 256],
                      writes=[wguB[wb]])
                K.dma(P_, wgu[wb][1], wu_d[l].rearrange("(k p) f -> p k f", p=128)[:, :, j * 256:(j + 1) * 256],
                      writes=[wguB[wb]])
                for fl in range(2):
                    fc = 2 * j + fl
                    for tg in range(TG):
                        pa, pb = (cnt % 2) * 2, (cnt % 2) * 2 + 1
                        tb = cnt % 2
                        cnt += 1
                        rd = [xnB[c] for c in range(tg * 4, min(CH, tg * 4 + 4))] + [wguB[wb]]
                        for (pi, wi) in ((pa, 0), (pb, 1)):
                            for k in range(8):
                                K.op(T, lambda e, pi=pi, wi=wi, k=k, fl=fl, tg=tg, wb=wb: e.matmul(
                                    ps[pi][:, 0:TW], lhsT=wgu[wb][wi][:, k, fl * 128:(fl + 1) * 128],
                                    rhs=xnT[:, k, tg * 512:tg * 512 + TW], start=(k == 0), stop=(k == 7)),
                                     reads=rd, writes=[psB[pi]])
                        K.op(A, lambda e, pa=pa, tb=tb: e.activation(out=tmp[tb][:, 0:TW], in_=ps[pa][:, 0:TW], func=AF.Silu),
                             reads=[psB[pa]], writes=[tmpB[tb]])
                        K.op(V, lambda e, pb=pb, tb=tb, fc=fc, tg=tg: e.tensor_tensor(
                            out=hT[:, fc, tg * 512:tg * 512 + TW], in0=ps[pb][:, 0:TW], in1=tmp[tb][:, 0:TW], op=ALU.mult),
                             reads=[psB[pb], tmpB[tb]], writes=[hB[fc][tg]])
            for dh in range(2):
                for c in range(CH):
                    t = ck * CH + c
                    pi = 4 + (c % 2)
                    tg = c // 4
                    for fc in range(NFC):
                        K.op(T, lambda e, pi=pi, fc=fc, c=c, dh=dh: e.matmul(
                            ps[pi][:, :], lhsT=hT[:, fc, c * 128:(c + 1) * 128], rhs=wd[dh][:, fc, :],
                            start=(fc == 0), stop=(fc == NFC - 1)), reads=[hB[fc][tg], wdB[dh]], writes=[psB[pi]])
                    tb = c % 2
                    K.op(V, lambda e, pi=pi, tb=tb, dh=dh: e.tensor_tensor(
                        out=tmp[tb][:, :], in0=ps[pi][:, :], in1=gate[:, dh * 512:(dh + 1) * 512], op=ALU.mult),
                         reads=[psB[pi], gateB], writes=[tmpB[tb]])
                    K.op(P_, lambda e, t=t, tb=tb, dh=dh: e.tensor_tensor(
                        out=x_sb[:, t, dh * 512:(dh + 1) * 512], in0=x_sb[:, t, dh * 512:(dh + 1) * 512], in1=tmp[tb][:, :],
                        op=ALU.add), reads=[tmpB[tb], xB[t]], writes=[xB[t]])


    def bcast_load(dst, src_row, B):
        K.dma(SP, dst, src_row.partition_broadcast(128), writes=[B])

    stc = [0, 0]

    def stcol(n=1):
        if n > 4:
            c = 64 + 24 * (stc[0] % 4)
            stc[0] += 1
        else:
            c = 160 + 4 * (stc[1] % 24)
            stc[1] += 1
        return c

    def norm_rope_gen(src, srcB, ng, gw, g_tile, gB, out_bf, outB, W3, rope=None):
        sq, t1, t2, wB = W3
        n = ng * gw
        v3 = lambda ap: ap.rearrange("p (g w) -> p g w", g=ng)
        c = stcol(3 * 8)
        ssq, vv, rs = st[:, c:c + ng], st[:, c + 8:c + 8 + ng], st[:, c + 16:c + 16 + ng]
        sB = stB[c]
        K.op(A, lambda e: e.activation(out=sq[:, 0:n], in_=src, func=AF.Square), reads=[srcB], writes=[wB[0]])
        yield
        K.op(V, lambda e: e.tensor_reduce(out=ssq, in_=v3(sq[:, 0:n]), axis=AX.X, op=ALU.add), reads=[wB[0]], writes=[sB])
        yield
        K.op(V, lambda e: e.tensor_scalar(out=vv, in0=ssq, scalar1=1.0 / gw, scalar2=EPS, op0=ALU.mult, op1=ALU.add),
             reads=[sB], writes=[sB])
        yield
        K.op(P_, lambda e: e.tensor_tensor(out=rs, in0=vv, in1=neghalf[:, 0:ng], op=ALU.pow), reads=[sB, cB], writes=[sB])
        yield
        K.op(V, lambda e: e.tensor_tensor(out=v3(t1[:, 0:n]), in0=v3(src), in1=rs.unsqueeze(2).broadcast_to([128, ng, gw]),
                                          op=ALU.mult), reads=[srcB, sB], writes=[wB[1]])
        yield
        gb3 = g_tile.unsqueeze(1).broadcast_to([128, ng, gw])
        if rope is None:
            K.op(P_, lambda e: e.tensor_tensor(out=v3(out_bf), in0=v3(t1[:, 0:n]), in1=gb3, op=ALU.mult),
                 reads=[wB[1], gB], writes=[outB])
            yield
            return
        r0, rh, cos_t, sin_t = rope
        K.op(P_, lambda e: e.tensor_tensor(out=v3(t1[:, 0:n]), in0=v3(t1[:, 0:n]), in1=gb3, op=ALU.mult),
             reads=[wB[1], gB], writes=[wB[1]])
        yield
        t13, t23, o3 = v3(t1[:, 0:n]), v3(t2[:, 0:n]), v3(out_bf)
        if r0 > 0:
            K.op(A, lambda e: e.activation(out=o3[:, :, 0:r0], in_=t13[:, :, 0:r0], func=AF.Copy), reads=[wB[1]], writes=[outB])
            yield
        cb = cos_t.unsqueeze(1).broadcast_to([128, ng, rh])
        sb_ = sin_t.unsqueeze(1).broadcast_to([128, ng, rh])
        x1, x2 = t13[:, :, r0:r0 + rh], t13[:, :, r0 + rh:r0 + 2 * rh]
        a_, b_ = t23[:, :, 0:rh], t23[:, :, rh:2 * rh]
        K.op(V, lambda e: e.tensor_tensor(out=a_, in0=x1, in1=cb, op=ALU.mult), reads=[wB[1], ropeB], writes=[wB[2]])
        yield
        K.op(P_, lambda e: e.tensor_tensor(out=b_, in0=x2, in1=sb_, op=ALU.mult), reads=[wB[1], ropeB], writes=[wB[3]])
        yield
        K.op(V, lambda e: e.tensor_tensor(out=o3[:, :, r0:r0 + rh], in0=a_, in1=b_, op=ALU.subtract),
             reads=[wB[2], wB[3]], writes=[outB])
        yield
        K.op(V, lambda e: e.tensor_tensor(out=a_, in0=x2, in1=cb, op=ALU.mult), reads=[wB[1], ropeB], writes=[wB[2]])
        yield
        K.op(P_, lambda e: e.tensor_tensor(out=b_, in0=x1, in1=sb_, op=ALU.mult), reads=[wB[1], ropeB], writes=[wB[3]])
        yield
        K.op(V, lambda e: e.tensor_tensor(out=o3[:, :, r0 + rh:r0 + 2 * rh], in0=a_, in1=b_, op=ALU.add),
             reads=[wB[2], wB[3]], writes=[outB])
        yield

    def run_chains(gens):
        gens = list(gens)
        while gens:
            for g_ in list(gens):
                try:
                    next(g_)
                except StopIteration:
                    gens.remove(g_)

    def norm_rope(src, srcB, ng, gw, g_tile, gB, out_bf, outB, W3, rope=None):
        run_chains([norm_rope_gen(src, srcB, ng, gw, g_tile, gB, out_bf, outB, W3, rope)])

    def transposes(src_bf, srcB, n, w, dst, dstB, rows=128, eng=A, bank=None):
        if bank is None:
            pi = trps[trcnt[0] % 2]; trcnt[0] += 1
        else:
            pi = bank
        pv = ps_bf(pi)
        for i in range(n):
            K.op(T, lambda e, i=i: e.transpose(pv[0:w, i * 128:i * 128 + rows], src_bf[0:rows, i * w:(i + 1) * w],
                                               ident_bf[0:rows, 0:rows]), reads=[srcB, cB], writes=[psB[pi]])
        src3 = pv[0:w, 0:n * 128].rearrange("p (a b) -> p a b", a=n)[:, :, 0:rows]
        if eng == A:
            K.op(A, lambda e: e.activation(out=dst, in_=src3, func=AF.Copy), reads=[psB[pi]], writes=[dstB])
        else:
            K.op(V, lambda e: e.tensor_copy(out=dst, in_=src3), reads=[psB[pi]], writes=[dstB])

    def inproj_block(hT, hB, w_sb, wB_, c0, wdt, pi):
        for k in range(8):
            K.op(T, lambda e, k=k: e.matmul(ps[pi][:, 0:wdt], lhsT=hT[:, k, :], rhs=w_sb[:, k, c0:c0 + wdt],
                                            start=(k == 0), stop=(k == 7)), reads=[hB, wB_], writes=[psB[pi]])

    def allgather(loc, locB, allt, allB):
        K.cc(lambda e: e.collective_compute("AllGather", ALU.bypass, replica_groups=[[0, 1], [2, 3], [4, 5], [6, 7]],
                                            ins=[loc.ap().opt()], outs=[allt.ap().opt()]), reads=[locB], writes=[allB])

    def ov_views(obanks, dv):
        dv1 = dv + 1
        return [ps[obanks[0]][:, 0:2 * dv1].rearrange("p (a b) -> p a b", a=2),
                ps[obanks[1]][:, 0:2 * dv1].rearrange("p (a b) -> p a b", a=2)]

    def attention_gen(G, kt, ktB_, qt, qtB_, vaug, vB_, r0, nr, dv, pT, pTB, pcount, sbanks=(0, 1), obanks=(2, 3)):
        dv1 = dv + 1
        blocks = [(kr, j) for kr in range(2) for j in range(4 * G + 4)]
        Ov = ov_views(obanks, dv)
        pend = None
        started = [False, False]

        def do_pv(item):
            n, kr, j, c0, pb = item
            for qb in range(c0 // 128, 4):
                bank = qb // 2
                st_ = not started[bank]
                started[bank] = True
                last = (kr == 1 and j == 4 * G + qb)
                K.op(T, lambda e, qb=qb, bank=bank, st_=st_, last=last, pb=pb, kr=kr, j=j: e.matmul(
                    Ov[bank][:, qb % 2, :], lhsT=pT[pb][:, qb * 128:(qb + 1) * 128], rhs=vaug[:, kr * NT + j, 0:dv1],
                    start=st_, stop=last, skip_group_check=True), reads=[pTB[pb], vB_], writes=[psB[obanks[bank]]])

        for n, (kr, j) in enumerate(blocks):
            jl = j - 4 * G
            c0 = max(0, jl) * 128
            s_ = sbanks[n % 2]
            diag = jl >= 0
            K.op(T, lambda e, s_=s_, c0=c0, kr=kr, j=j, diag=diag: e.matmul(
                ps[s_][:, c0:512], lhsT=kt[r0:r0 + nr, kr, j * 128:(j + 1) * 128],
                rhs=qt[r0:r0 + nr, G * 512 + c0:G * 512 + 512], start=True, stop=not diag, skip_group_check=True),
                 reads=[ktB_, qtB_], writes=[psB[s_]])
            if diag:
                K.op(T, lambda e, s_=s_, c0=c0, kr=kr, jl=jl: e.matmul(
                    ps[s_][:, c0:c0 + 128], lhsT=ident_bf[:, :], rhs=masks[:, kr * 2 + (jl % 2), :], start=False, stop=True,
                    skip_group_check=True), reads=[cB], writes=[psB[s_]])
            pb = pcount[0] % len(pT); pcount[0] += 1
            K.op(A, lambda e, s_=s_, c0=c0, pb=pb: e.activation(out=pT[pb][:, c0:512], in_=ps[s_][:, c0:512], func=AF.Exp),
                 reads=[psB[s_]], writes=[pTB[pb]])
            if pend is not None:
                do_pv(pend)
            pend = (n, kr, j, c0, pb)
            yield
        do_pv(pend)

    def attention_map(G, kt, ktB_, qt, qtB_, vaug, vB_, r0, nr, dv, pT, pTB, pcount):
        run_chains([attention_gen(G, kt, ktB_, qt, qtB_, vaug, vB_, r0, nr, dv, pT, pTB, pcount)])
        return ov_views((2, 3), dv)

    def load_kt(kt, ktB_, all_list, hh, g, rows):
        for kr in range(2):
            K.dma(SP, kt[0:rows, kr, :], all_list[g][(kr * (all_list[g].shape[0] // (2 * rows)) + hh) * rows:
                                                     (kr * (all_list[g].shape[0] // (2 * rows)) + hh + 1) * rows, :],
                  reads=[allB[id(all_list[g])]], writes=[ktB_])

    def load_v(vaug, vB_, v_all, c0, dv):
        npv = NT // NVH
        for vg in range(NVH):
            for kr in range(2):
                K.dma(SP, vaug[:, kr * NT + vg * npv:kr * NT + (vg + 1) * npv, 0:dv],
                      v_all[vg][kr * VR:(kr + 1) * VR, c0:c0 + dv].rearrange("(j p) d -> p j d", p=128),
                      reads=[allB[id(v_all[vg])]], writes=[vB_])

    allB = {}

    def regB(t):
        allB[id(t)] = Buf("dram")
        return allB[id(t)]

    def mixer0():
        li = 0.8 - 0.6 * math.exp(-0.3 * 0)
        K.phase = "m0_inproj"
        compute_mod(0, 1, 1.0)
        al = Bump()
        w_in = al([128, 8, 3744], BF16); w_inB = Buf("w_in")
        w_qb = al([128, 3, 768], BF16); w_kvb = al([128, 2, 1536], BF16); wsB = Buf("w_small")
        hTt = [al([128, 8, 128], BF16) for _ in range(2)]; hTB = [Buf("hT0"), Buf("hT1")]
        gq = al([128, 64], F32); gk = al([128, 64], F32); gqb = al([128, 96], F32); gkb = al([128, 96], F32)
        gql = al([128, 384], F32); gkvl = al([128, 256], F32); gB = Buf("gains")
        W3s = [(al([128, 512], F32), al([128, 512], F32), al([128, 512], F32), [Buf(f"sq{i}"), Buf(f"t1{i}"), Buf(f"t2a{i}"), Buf(f"t2b{i}")])
               for i in range(4)]
        W3 = W3s[0]
        qk_bf = [al([128, 512], BF16) for _ in range(4)]; qkB = [Buf(f"qk{i}") for i in range(4)]
        qtA_t = al([128, 8, 128], BF16); ktA_t = al([128, 8, 128], BF16)
        qtB_t = al([128, 8, 128], BF16); ktB_t = al([128, 8, 128], BF16)
        stgB = [Buf("qtA_t"), Buf("ktA_t"), Buf("qtB_t"), Buf("ktB_t")]
        va_bf = al([128, 1024], BF16); vb_bf = al([128, 1024], BF16); vaB = Buf("va"); vbB = Buf("vb")
        cq_bf = al([128, 384], BF16); ckv_bf = al([128, 256], BF16); cqB = Buf("cq"); ckvB = Buf("ckv")
        cqT = al([128, 3, 128], BF16); ckvT = al([128, 2, 128], BF16); cqTB = Buf("cqT"); ckvTB = Buf("ckvT")
        kb_f = al([128, 768], F32); kbB = Buf("kb_f")
        for (dst, src) in ((gq, diff_q_g_d[0:1, :]), (gk, diff_k_g_d[0:1, :]), (gqb, mla_q_g_d[0:1, :]), (gkb, mla_k_g_d[0:1, :]),
                           (gql, q_lat_g_d[0:1, :]), (gkvl, kv_lat_g_d[0:1, :])):
            bcast_load(dst, src, gB)
        K.op(V, lambda e: e.tensor_scalar(out=gq, in0=gq, scalar1=64 ** -0.5, scalar2=None, op0=ALU.mult), reads=[gB], writes=[gB])
        K.op(V, lambda e: e.tensor_scalar(out=gqb, in0=gqb, scalar1=96 ** -0.5, scalar2=None, op0=ALU.mult), reads=[gB], writes=[gB])
        for c0 in range(0, 3744, 468):
            K.dma(P_, w_in[:, :, c0:c0 + 468], ab_w_in_d[0].rearrange("(k p) c -> p k c", p=128)[:, :, c0:c0 + 468], writes=[w_inB])
        K.dma(P_, w_qb, w_qb_d[0].rearrange("(k p) c -> p k c", p=128), writes=[wsB])
        K.dma(P_, w_kvb, w_kvb_d[0].rearrange("(k p) c -> p k c", p=128), writes=[wsB])
        for tl in (qtA_s, qtB_s, attn0_s) + tuple(ktA_loc + ktA_all + ktB_loc + ktB_all + vA_loc + vA_all + vB_loc + vB_all):
            regB(tl)
        npv = NT // NVH
        pcnt = 0
        import os as _os
        SEC = _os.environ.get("MIX0_SEC", "abcd")
        for t in range(NT):
            hT = hTt[t % 2]; hB_ = hTB[t % 2]
            mod_transpose(t, hT, 0, hB_)
            tok = slice(t * 128, (t + 1) * 128)
            specs = ((0, gq, qtA_t, stgB[0], True), (1024, gk, ktA_t, stgB[1], False))
            chains = []
            for si_, (base, g_t, st_t, stb, is_q) in enumerate(specs):
                for blk in range(2):
                    ci = si_ * 2 + blk
                    pi = ci
                    inproj_block(hT, hB_, w_in, w_inB, base + blk * 512, 512, pi)
                    chains.append(norm_rope_gen(ps[pi][:, 0:512], psB[pi], 8, 64, g_t, gB, qk_bf[ci], qkB[ci], W3s[ci],
                                                rope=(0, 32, cosA[:, t, :], sinA[:, t, :])))
            run_chains(chains)
            pcnt = 0
            for si_, (base, g_t, st_t, stb, is_q) in enumerate(specs):
                for blk in range(2):
                    ci = si_ * 2 + blk
                    transposes(qk_bf[ci], qkB[ci], 4, 128, st_t[:, blk * 4:(blk + 1) * 4, :], stb)
                if is_q:
                    K.dma(SP, qtA_s.ap().rearrange("h p s -> p h s")[:, :, tok], st_t, reads=[stb], writes=[allB[id(qtA_s)]])
                else:
                    for g in range(2):
                        K.dma(SP, ktA_loc[g].ap().rearrange("(h p) s -> p h s", p=128)[:, :, tok], st_t[:, g * 4:(g + 1) * 4, :],
                              reads=[stb], writes=[allB[id(ktA_loc[g])]])
            for blk in (range(2) if "b" in SEC else ()):
                pi = pcnt % 4; pcnt += 1
                inproj_block(hT, hB_, w_in, w_inB, 2048 + blk * 512, 512, pi)
                K.op(A, lambda e, pi=pi, blk=blk: e.activation(out=va_bf[:, blk * 512:(blk + 1) * 512], in_=ps[pi][:, :], func=AF.Copy),
                     reads=[psB[pi]], writes=[vaB])
            if "b" in SEC:
                K.dma(SP, vA_loc[t // npv][(t % npv) * 128:(t % npv + 1) * 128, :], va_bf, reads=[vaB], writes=[allB[id(vA_loc[t // npv])]])
            if "c" not in SEC and "d" not in SEC:
                continue
            pq = pcnt % 4; pcnt += 1
            inproj_block(hT, hB_, w_in, w_inB, 3072, 384, pq)
            pk = pcnt % 4; pcnt += 1
            inproj_block(hT, hB_, w_in, w_inB, 3456, 288, pk)
            run_chains([norm_rope_gen(ps[pq][:, 0:384], psB[pq], 1, 384, gql, gB, cq_bf, cqB, W3s[0]),
                        norm_rope_gen(ps[pk][:, 0:256], psB[pk], 1, 256, gkvl, gB, ckv_bf, ckvB, W3s[1])])
            kb3 = kb_f.rearrange("p (h w) -> p h w", h=8)
            K.op(V, lambda e, pk=pk: e.tensor_copy(out=kb3[:, :, 64:96], in_=ps[pk][:, 256:288].unsqueeze(1).broadcast_to([128, 8, 32])),
                 reads=[psB[pk]], writes=[kbB])
            transposes(cq_bf, cqB, 3, 128, cqT, cqTB)
            transposes(ckv_bf, ckvB, 2, 128, ckvT, ckvTB)
            for blk in range(4):
                pi = pcnt % 4; pcnt += 1
                for k in range(2):
                    K.op(T, lambda e, k=k, pi=pi, blk=blk: e.matmul(ps[pi][:, 0:384], lhsT=ckvT[:, k, :], rhs=w_kvb[:, k, blk * 384:(blk + 1) * 384],
                                                                   start=(k == 0), stop=(k == 1)), reads=[ckvTB, wsB], writes=[psB[pi]])
                p3 = ps[pi][:, 0:384].rearrange("p (h w) -> p h w", h=2)
                K.op(V, lambda e, p3=p3, blk=blk: e.tensor_copy(out=kb3[:, blk * 2:blk * 2 + 2, 0:64], in_=p3[:, :, 0:64]),
                     reads=[psB[pi]], writes=[kbB])
                K.op(A, lambda e, p3=p3, blk=blk: e.activation(out=vb_bf[:, blk * 256:(blk + 1) * 256].rearrange("p (h w) -> p h w", h=2),
                                                               in_=p3[:, :, 64:192], func=AF.Copy), reads=[psB[pi]], writes=[vbB])
            chains = []
            pis = []
            for blk in range(2):
                pi = pcnt % 4; pcnt += 1
                pis.append(pi)
                for k in range(3):
                    K.op(T, lambda e, k=k, pi=pi, blk=blk: e.matmul(ps[pi][:, 0:384], lhsT=cqT[:, k, :], rhs=w_qb[:, k, blk * 384:(blk + 1) * 384],
                                                                   start=(k == 0), stop=(k == 2)), reads=[cqTB, wsB], writes=[psB[pi]])
                chains.append(norm_rope_gen(ps[pi][:, 0:384], psB[pi], 4, 96, gqb, gB, qk_bf[blk][:, 0:384], qkB[blk], W3s[blk],
                                            rope=(64, 16, cosB[:, t, :], sinB[:, t, :])))
            for blk in range(2):
                chains.append(norm_rope_gen(kb_f[:, blk * 384:(blk + 1) * 384], kbB, 4, 96, gkb, gB, qk_bf[2 + blk][:, 0:384], qkB[2 + blk], W3s[2 + blk],
                                            rope=(64, 16, cosB[:, t, :], sinB[:, t, :])))
            run_chains(chains)
            for blk in range(2):
                transposes(qk_bf[blk], qkB[blk], 4, 96, qtB_t[0:96, blk * 4:(blk + 1) * 4, :], stgB[2])
            K.dma(SP, qtB_s.ap().rearrange("h p s -> p h s")[:, :, tok], qtB_t[0:96, :, :], reads=[stgB[2]], writes=[allB[id(qtB_s)]])
            for blk in range(2):
                transposes(qk_bf[2 + blk], qkB[2 + blk], 4, 96, ktB_t[0:96, blk * 4:(blk + 1) * 4, :], stgB[3])
            for g in range(2):
                K.dma(SP, ktB_loc[g].ap().rearrange("(h p) s -> p h s", p=96)[:, :, tok], ktB_t[0:96, g * 4:(g + 1) * 4, :],
                      reads=[stgB[3]], writes=[allB[id(ktB_loc[g])]])
            K.dma(SP, vB_loc[t // npv][(t % npv) * 128:(t % npv + 1) * 128, :], vb_bf, reads=[vbB], writes=[allB[id(vB_loc[t // npv])]])
        if stop == "m0_inproj":
            return
        K.phase = "m0_exchange"
        for g in range(2):
            allgather(ktA_loc[g], allB[id(ktA_loc[g])], ktA_all[g], allB[id(ktA_all[g])])
        for g in range(NVH):
            allgather(vA_loc[g], allB[id(vA_loc[g])], vA_all[g], allB[id(vA_all[g])])
        for g in range(2):
            allgather(ktB_loc[g], allB[id(ktB_loc[g])], ktB_all[g], allB[id(ktB_all[g])])
        for g in range(NVH):
            allgather(vB_loc[g], allB[id(vB_loc[g])], vB_all[g], allB[id(vB_all[g])])
        if stop == "m0_cc":
            return
        K.phase = "m0_attn"
        al = Bump()
        kt = [al([128, 2, S_OWN], BF16) for _ in range(2)]; ktBs = [Buf("kt0"), Buf("kt1")]
        qt = [al([128, S_OWN], BF16) for _ in range(2)]; qtBs = [Buf("qt0"), Buf("qt1")]
        vaug = [al([128, 2 * NT, 129], BF16) for _ in range(2)]; vBs = [Buf("v0"), Buf("v1")]
        pT = [al([128, 512], BF16) for _ in range(8)]; pTB = [Buf(f"pT{i}") for i in range(8)]
        on = [al([128, 4, 128], F32) for _ in range(2)]; onB = [Buf("on0"), Buf("on1")]
        o_f = al([128, 128], F32); o_fB = Buf("o_f")
        ob4 = al([128, 512], BF16); ob4B = Buf("ob4")
        ob4b = al([128, 512], BF16); ob4bB = Buf("ob4b")
        oT = [al([128, 512], BF16) for _ in range(2)]; oTB = [Buf("oT0"), Buf("oT1")]
        lamv = al([128, 4, 64], F32); lamB = Buf("lam")
        sg = al([128, 128], F32); sgB = Buf("sg")
        for i in range(2):
            K.op(P_, lambda e, i=i: e.memset(vaug[i][:, :, 128:129], 1.0), writes=[vBs[i]])
        for i in range(4):
            bcast_load(lamv[:, i, :], lam_d[i][0:1, :], lamB)
        bcast_load(sg, subln_g_d[0:1, :], sgB)
        K.op(V, lambda e: e.tensor_scalar(out=sg, in0=sg, scalar1=1.0 - li, scalar2=None, op0=ALU.mult), reads=[sgB], writes=[sgB])
        LC = 8
        for i in range(2):
            K.op(V, lambda e, i=i: e.scalar_tensor_tensor(out=junk[:, 0:64], in0=lamv[:, 2 * i, :], scalar=1.0, in1=lamv[:, 2 * i + 1, :],
                                                          op0=ALU.mult, op1=ALU.mult, accum_out=st[:, 60 + i:61 + i]),
                 reads=[lamB], writes=[junkB, stB[60]])
        K.op(A, lambda e: e.activation(out=st[:, 60:62], in_=st[:, 60:62], func=AF.Exp), reads=[stB[60]], writes=[stB[60]])
        K.op(V, lambda e: e.scalar_tensor_tensor(out=st[:, 62:63], in0=st[:, 61:62], scalar=-li, in1=st[:, 60:61],
                                                 op0=ALU.add, op1=ALU.subtract), reads=[stB[60]], writes=[stB[60]])
        pcount = [0]
        ocnt = 0
        def load_head0(h):
            bi = h % 2
            g, hh = (h % 8) // 4, (h % 8) % 4
            if h < 8:
                load_kt(kt[bi], ktBs[bi], ktA_all, hh, g, 128)
                K.dma(SP, qt[bi][:, :], qtA_s[h], reads=[allB[id(qtA_s)]], writes=[qtBs[bi]])
                load_v(vaug[bi], vBs[bi], vA_all, h * 128, 128)
            else:
                load_kt(kt[bi], ktBs[bi], ktB_all, hh, g, 96)
                K.dma(SP, qt[bi][0:96, :], qtB_s[h - 8], reads=[allB[id(qtB_s)]], writes=[qtBs[bi]])
                load_v(vaug[bi], vBs[bi], vB_all, (h - 8) * 128, 128)

        pending0 = []
        ob4c = [(ob4, ob4B), (ob4b, ob4bB)]
        load_head0(0)
        for h in range(16):
            is_diff = h < 8
            bi = h % 2
            if h + 1 < 16:
                load_head0(h + 1)
            def post_diff(G, Ovs):
                for mi in range(2):
                    Ov = Ovs[mi]
                    ob_ = (2, 3) if mi == 0 else (6, 7)
                    for qb in range(4):
                        c = stcol(1)
                        K.op(V, lambda e, qb=qb, c=c, Ov=Ov: e.reciprocal(out=st[:, c:c + 1], in_=Ov[qb // 2][:, qb % 2, 128:129]),
                             reads=[psB[ob_[qb // 2]]], writes=[stB[c]])
                        K.op(V, lambda e, qb=qb, c=c, mi=mi, Ov=Ov: e.tensor_scalar(out=on[mi][:, qb, :], in0=Ov[qb // 2][:, qb % 2, 0:128],
                                                                                   scalar1=st[:, c:c + 1], scalar2=None, op0=ALU.mult),
                             reads=[psB[ob_[qb // 2]], stB[c]], writes=[onB[mi]])
                for qb in range(4):
                    c = stcol(3)
                    K.op(V, lambda e, qb=qb: e.scalar_tensor_tensor(out=o_f, in0=on[1][:, qb, :], scalar=st[:, 62:63], in1=on[0][:, qb, :],
                                                                    op0=ALU.mult, op1=ALU.add), reads=[onB[0], onB[1], stB[60]], writes=[o_fB])
                    K.op(V, lambda e, c=c: e.scalar_tensor_tensor(out=junk[:, 0:128], in0=o_f, scalar=1.0, in1=o_f, op0=ALU.mult, op1=ALU.mult,
                                                                  accum_out=st[:, c:c + 1]), reads=[o_fB], writes=[junkB, stB[c]])
                    K.op(V, lambda e, c=c: e.tensor_scalar(out=st[:, c + 1:c + 2], in0=st[:, c:c + 1], scalar1=1.0 / 128, scalar2=EPS,
                                                           op0=ALU.mult, op1=ALU.add), reads=[stB[c]], writes=[stB[c]])
                    K.op(P_, lambda e, c=c: e.tensor_tensor(out=st[:, c + 2:c + 3], in0=st[:, c + 1:c + 2], in1=neghalf[:, 0:1], op=ALU.pow),
                         reads=[stB[c], cB], writes=[stB[c]])
                    dst_, dstB_ = ob4c[G % 2]
                    K.op(V, lambda e, c=c, qb=qb, dst_=dst_: e.scalar_tensor_tensor(out=dst_[:, qb * 128:(qb + 1) * 128], in0=o_f, scalar=st[:, c + 2:c + 3],
                                                                                   in1=sg, op0=ALU.mult, op1=ALU.mult),
                         reads=[o_fB, stB[c], sgB], writes=[dstB_])

            def post_mla(Ov, ob_, dst, dstB):
                for qb in range(4):
                    c = stcol(1)
                    K.op(V, lambda e, qb=qb, c=c: e.reciprocal(out=st[:, c:c + 1], in_=Ov[qb // 2][:, qb % 2, 128:129]),
                         reads=[psB[ob_[qb // 2]]], writes=[stB[c]])
                    K.op(V, lambda e, qb=qb, c=c: e.tensor_scalar(out=dst[:, qb * 128:(qb + 1) * 128], in0=Ov[qb // 2][:, qb % 2, 0:128],
                                                                  scalar1=st[:, c:c + 1], scalar2=None, op0=ALU.mult),
                         reads=[psB[ob_[qb // 2]], stB[c]], writes=[dstB])

            def emit_out(G, src=None, srcB=None, h=h, bank=0):
                nonlocal ocnt
                src = ob4 if src is None else src
                srcB = ob4B if srcB is None else srcB
                oi = ocnt % 2; ocnt += 1
                transposes(src, srcB, 4, 128, oT[oi].rearrange("p (a b) -> p a b", a=4), oTB[oi], bank=bank)
                K.dma(SP, attn0_s[h][:, G * 512:(G + 1) * 512], oT[oi], reads=[oTB[oi]], writes=[allB[id(attn0_s)]])

            def flush():
                for f_ in pending0:
                    f_()
                pending0.clear()

            args = (kt[bi], ktBs[bi], qt[bi], qtBs[bi], vaug[bi], vBs[bi])
            if is_diff:
                for G in range(NG):
                    run_chains([attention_gen(G, *args, 0, 64, 128, pT, pTB, pcount, (0, 1), (2, 3)),
                                attention_gen(G, *args, 64, 64, 128, pT, pTB, pcount, (4, 5), (6, 7))])
                    flush()
                    post_diff(G, [ov_views((2, 3), 128), ov_views((6, 7), 128)])
                    pending0.append(lambda G=G, eo=emit_out: eo(G, ob4c[G % 2][0], ob4c[G % 2][1]))
            else:
                for G0 in range(0, NG, 2):
                    Gs = [G_ for G_ in (G0, G0 + 1) if G_ < NG]
                    bankset = [((0, 1), (2, 3)), ((4, 5), (6, 7))]
                    run_chains([attention_gen(G_, *args, 0, 96, 128, pT, pTB, pcount, *bankset[i]) for i, G_ in enumerate(Gs)])
                    flush()
                    obs = [(ob4, ob4B), (ob4b, ob4bB)]
                    for i, G_ in enumerate(Gs):
                        post_mla(ov_views(bankset[i][1], 128), bankset[i][1], *obs[i])
                    for i, G_ in enumerate(Gs):
                        pending0.append(lambda G_=G_, i=i, eo=emit_out, obs=obs: eo(G_, obs[i][0], obs[i][1], bank=4 * i))
        for f_ in pending0:
            f_()
        pending0.clear()
        out_proj(attn0_s, allB[id(attn0_s)], ab_w_out_d[0], 16)

    def mixer1():
        K.phase = "m1_inproj"
        compute_mod(1, 1, 1.0)
        al = Bump()
        w_in = al([128, 8, 4112], BF16); w_inB = Buf("w_in1")
        hTt = [al([128, 8, 128], BF16) for _ in range(2)]; hTB = [Buf("hT0"), Buf("hT1")]
        gq = al([128, 64], F32); gk = al([128, 64], F32); bf_t = al([128, 16], F32); gB = Buf("gains1")
        W3s = [(al([128, 512], F32), al([128, 512], F32), None, [Buf(f"sq{i}"), Buf(f"t1{i}"), Buf(f"t2a{i}"), Buf(f"t2b{i}")]) for i in range(4)]
        qk_bf = [al([128, 512], BF16) for _ in range(4)]; qkB = [Buf(f"qk{i}") for i in range(4)]
        qt_t = al([128, 16, 128], BF16); kt_t = al([128, 16, 128], BF16); stgB = [Buf("qt_t"), Buf("kt_t")]
        v_bf = al([128, 1024], BF16); og_bf = al([128, 1024], BF16); vB1 = Buf("v1"); ogB = Buf("og")
        lf_tok = al([128, NT, 16], F32); lfB = Buf("lf_tok")
        zt = al([128, 16], F32); ztB = Buf("zt")
        bcast_load(gq, fox_q_g_d[0:1, :], gB)
        bcast_load(gk, fox_k_g_d[0:1, :], gB)
        bcast_load(bf_t, fox_b_f_d[0:1, :], gB)
        K.op(V, lambda e: e.tensor_scalar(out=gq, in0=gq, scalar1=64 ** -0.5, scalar2=None, op0=ALU.mult), reads=[gB], writes=[gB])
        for c0 in range(0, 4112, 514):
            K.dma(P_, w_in[:, :, c0:c0 + 514], fox_w_in_d[0].rearrange("(k p) c -> p k c", p=128)[:, :, c0:c0 + 514], writes=[w_inB])
        for tl in (qtF_s, attn1_s, og_s, lf_loc, lf_all) + tuple(ktF_loc + ktF_all + vF_loc + vF_all):
            regB(tl)
        K.op(P_, lambda e: e.memset(qt_t[64:70, :, :], 1.0), writes=[stgB[0]])
        K.op(P_, lambda e: e.memset(kt_t[64:70, :, :], 1.0), writes=[stgB[1]])
        npv = NT // NVH
        pcnt = 0
        for t in range(NT):
            hT = hTt[t % 2]; hB_ = hTB[t % 2]
            mod_transpose(t, hT, 0, hB_)
            tok = slice(t * 128, (t + 1) * 128)
            chains = []
            for ci, (base, g_t, blk) in enumerate(((0, gq, 0), (0, gq, 1), (1024, gk, 0), (1024, gk, 1))):
                pi = pcnt % 4; pcnt += 1
                inproj_block(hT, hB_, w_in, w_inB, base + blk * 512, 512, pi)
                chains.append(norm_rope_gen(ps[pi][:, 0:512], psB[pi], 8, 64, g_t, gB, qk_bf[ci], qkB[ci], W3s[ci]))
            run_chains(chains)
            for (base, g_t, st_t, stb, is_q) in ((0, gq, qt_t, stgB[0], True), (1024, gk, kt_t, stgB[1], False)):
                for blk in range(2):
                    ci = (0 if is_q else 2) + blk
                    transposes(qk_bf[ci], qkB[ci], 8, 64, st_t[0:64, blk * 8:(blk + 1) * 8, :], stb)
                if is_q:
                    K.dma(SP, qtF_s.ap().rearrange("h p s -> p h s")[:, :, tok], st_t[0:70, :, :], reads=[stb], writes=[allB[id(qtF_s)]])
                else:
                    for g in range(4):
                        K.dma(SP, ktF_loc[g].ap().rearrange("(h p) s -> p h s", p=70)[:, :, tok], st_t[0:70, g * 4:(g + 1) * 4, :],
                              reads=[stb], writes=[allB[id(ktF_loc[g])]])
            for blk in range(2):
                pi = pcnt % 4; pcnt += 1
                inproj_block(hT, hB_, w_in, w_inB, 2048 + blk * 512, 512, pi)
                K.op(A, lambda e, pi=pi, blk=blk: e.activation(out=v_bf[:, blk * 512:(blk + 1) * 512], in_=ps[pi][:, :], func=AF.Copy),
                     reads=[psB[pi]], writes=[vB1])
            K.dma(SP, vF_loc[t // npv][(t % npv) * 128:(t % npv + 1) * 128, :], v_bf, reads=[vB1], writes=[allB[id(vF_loc[t // npv])]])
            for blk in range(2):
                pi = pcnt % 4; pcnt += 1
                inproj_block(hT, hB_, w_in, w_inB, 3072 + blk * 512, 512, pi)
                K.op(A, lambda e, pi=pi, blk=blk: e.activation(out=og_bf[:, blk * 512:(blk + 1) * 512], in_=ps[pi][:, :], func=AF.Sigmoid),
                     reads=[psB[pi]], writes=[ogB])
            K.dma(SP, og_s[tok, :], og_bf, reads=[ogB], writes=[allB[id(og_s)]])
            pi = pcnt % 4; pcnt += 1
            inproj_block(hT, hB_, w_in, w_inB, 4096, 16, pi)
            K.op(V, lambda e, pi=pi: e.tensor_tensor(out=zt, in0=ps[pi][:, 0:16], in1=bf_t, op=ALU.add), reads=[psB[pi], gB], writes=[ztB])
            K.op(A, lambda e: e.activation(out=zt, in_=zt, func=AF.Exp, scale=-1.0), reads=[ztB], writes=[ztB])
            K.op(V, lambda e: e.tensor_scalar(out=zt, in0=zt, scalar1=1.0, scalar2=None, op0=ALU.add), reads=[ztB], writes=[ztB])
            K.op(A, lambda e, t=t: e.activation(out=lf_tok[:, t, :], in_=zt, func=AF.Ln), reads=[ztB], writes=[lfB])
        K.phase = "m1_exchange"
        for g in range(4):
            allgather(ktF_loc[g], allB[id(ktF_loc[g])], ktF_all[g], allB[id(ktF_all[g])])
        for g in range(NVH):
            allgather(vF_loc[g], allB[id(vF_loc[g])], vF_all[g], allB[id(vF_all[g])])
        al = Bump()
        lfT = al([16, S_OWN], F32); lfTB = Buf("lfT")
        lf_rm = al([16, 2, S_OWN], F32); lf_rmB = Buf("lf_rm")
        cg = al([16, S_ALL], F32); cgB = Buf("cg")
        ones_f = al([16, S_ALL], F32); onesB = Buf("ones_f")
        hi = al([16, 2, S_OWN], BF16); mid = al([16, 2, S_OWN], BF16); lo = al([16, 2, S_OWN], BF16); splB = Buf("split")
        r1 = al([16, 2, S_OWN], F32); r1B = Buf("r1")
        co = al([16, S_OWN], F32); coB = Buf("co")
        qh = al([16, 3, S_OWN], BF16); qhB = Buf("qh")
        sel = al([128, 2], F32); selB = Buf("sel")
        K.dma(SP, sel, sel_d.partition_broadcast(128), writes=[selB])
        K.op(V, lambda e: e.tensor_scalar(out=sel, in0=sel, scalar1=-1.0, scalar2=None, op0=ALU.mult), reads=[selB], writes=[selB])
        K.op(P_, lambda e: e.memset(ones_f, 1.0), writes=[onesB])
        for t0 in range(0, NT, 4):
            pi = 4 + (t0 // 4) % 2
            for t in range(t0, min(NT, t0 + 4)):
                K.op(T, lambda e, t=t, t0=t0, pi=pi: e.transpose(ps[pi][0:16, (t - t0) * 128:(t - t0 + 1) * 128], lf_tok[:, t, :], ident_f[:, :]),
                     reads=[lfB, cB], writes=[psB[pi]])
            n_ = min(NT, t0 + 4) - t0
            K.op(V, lambda e, t0=t0, pi=pi, n_=n_: e.tensor_copy(out=lfT[:, t0 * 128:(t0 + n_) * 128], in_=ps[pi][0:16, 0:n_ * 128]),
                 reads=[psB[pi]], writes=[lfTB])
        K.dma(SP, lf_loc.ap(), lfT, reads=[lfTB], writes=[allB[id(lf_loc)]])
        allgather(lf_loc, allB[id(lf_loc)], lf_all, allB[id(lf_all)])
        K.dma(SP, lf_rm, lf_all.ap().rearrange("(r h) s -> h r s", r=2), reads=[allB[id(lf_all)]], writes=[lf_rmB])
        lf4 = lf_rm.rearrange("h r (i p) -> h r i p", p=128)
        cg3 = cg.rearrange("h (b p) -> h b p", p=128)
        hn = NT // 2

        def perm_pairs():
            return [(0, 0, 0), (0, 1, 3), (1, 0, 1), (1, 1, 2)]

        for (r, e_, gs) in perm_pairs():
            for m in range(hn):
                K.op(V, lambda e, r=r, e_=e_, gs=gs, m=m: e.tensor_copy(out=cg3[:, gs + 4 * m, :], in_=lf4[:, r, e_ + 2 * m, :]),
                     reads=[lf_rmB], writes=[cgB])
        K.op(V, lambda e: e.tensor_tensor_scan(out=cg, data0=ones_f, data1=cg, initial=0.0, op0=ALU.mult, op1=ALU.add),
             reads=[cgB, onesB], writes=[cgB])
        for (r, e_, gs) in perm_pairs():
            for m in range(hn):
                K.op(V, lambda e, r=r, e_=e_, gs=gs, m=m: e.tensor_copy(out=lf4[:, r, e_ + 2 * m, :], in_=cg3[:, gs + 4 * m, :]),
                     reads=[cgB], writes=[lf_rmB])
        K.op(V, lambda e: e.tensor_copy(out=hi, in_=lf_rm), reads=[lf_rmB], writes=[splB])
        K.op(V, lambda e: e.tensor_tensor(out=r1, in0=lf_rm, in1=hi, op=ALU.subtract), reads=[lf_rmB, splB], writes=[r1B])
        K.op(V, lambda e: e.tensor_copy(out=mid, in_=r1), reads=[r1B], writes=[splB])
        K.op(V, lambda e: e.tensor_tensor(out=r1, in0=r1, in1=mid, op=ALU.subtract), reads=[r1B, splB], writes=[r1B])
        K.op(V, lambda e: e.tensor_copy(out=lo, in_=r1), reads=[r1B], writes=[splB])
        for g in range(4):
            kv = ktF_all[g].ap().rearrange("(r h w) s -> h r w s", r=2, h=4)
            for kr in range(2):
                for ci, comp in enumerate((hi, mid, lo)):
                    K.dma(SP, kv[:, kr, 67 + ci, :], comp[g * 4:(g + 1) * 4, kr, :], reads=[splB], writes=[allB[id(ktF_all[g])]])
        K.op(V, lambda e: e.tensor_scalar(out=co, in0=lf_rm[:, 0, :], scalar1=sel[0:16, 0:1], scalar2=None, op0=ALU.mult),
             reads=[lf_rmB, selB], writes=[coB])
        K.op(V, lambda e: e.scalar_tensor_tensor(out=co, in0=lf_rm[:, 1, :], scalar=sel[0:16, 1:2], in1=co, op0=ALU.mult, op1=ALU.add),
             reads=[lf_rmB, selB, coB], writes=[coB])
        r1q = r1[:, 0, :]
        K.op(V, lambda e: e.tensor_copy(out=qh[:, 0, :], in_=co), reads=[coB], writes=[qhB])
        K.op(V, lambda e: e.tensor_tensor(out=r1q, in0=co, in1=qh[:, 0, :], op=ALU.subtract), reads=[coB, qhB], writes=[r1B])
        K.op(V, lambda e: e.tensor_copy(out=qh[:, 1, :], in_=r1q), reads=[r1B], writes=[qhB])
        K.op(V, lambda e: e.tensor_tensor(out=r1q, in0=r1q, in1=qh[:, 1, :], op=ALU.subtract), reads=[r1B, qhB], writes=[r1B])
        K.op(V, lambda e: e.tensor_copy(out=qh[:, 2, :], in_=r1q), reads=[r1B], writes=[qhB])
        K.dma(SP, qtF_s[:, 64:67, :], qh, reads=[qhB], writes=[allB[id(qtF_s)]])
        K.phase = "m1_attn"
        al = Bump()
        kt = [al([70, 2, S_OWN], BF16) for _ in range(4)]; ktBs = [Buf(f"kt{i}") for i in range(4)]
        qt = [al([70, S_OWN], BF16) for _ in range(4)]; qtBs = [Buf(f"qt{i}") for i in range(4)]
        vaug = [al([128, 2 * NT, 65], BF16) for _ in range(4)]; vBs = [Buf(f"v{i}") for i in range(4)]
        pT = [al([128, 512], BF16) for _ in range(8)]; pTB = [Buf(f"pT{i}") for i in range(8)]
        obp2 = [al([128, 4, 128], BF16) for _ in range(2)]; obp2B = [Buf("obp0"), Buf("obp1")]
        pending1 = []
        og_t = [al([128, 4, 128], BF16) for _ in range(2)]; og_tB = [Buf("og_t0"), Buf("og_t1")]
        oT = [al([128, 512], BF16) for _ in range(2)]; oTB = [Buf("oT0"), Buf("oT1")]
        for i in range(4):
            K.op(P_, lambda e, i=i: e.memset(vaug[i][:, :, 64:65], 1.0), writes=[vBs[i]])
        pcount = [0]
        ocnt = 0

        def load_pair1(hp):
            for hl in range(2):
                h = 2 * hp + hl
                b = (hp % 2) * 2 + hl
                load_kt(kt[b], ktBs[b], ktF_all, h % 4, h // 4, 70)
                K.dma(SP, qt[b], qtF_s[h], reads=[allB[id(qtF_s)]], writes=[qtBs[b]])
                load_v(vaug[b], vBs[b], vF_all, h * 64, 64)

        load_pair1(0)
        for hp in range(8):
            if hp + 1 < 8:
                load_pair1(hp + 1)
            for G in range(NG):
                oi = ocnt % 2; ocnt += 1
                obp = obp2[oi]; obpB = obp2B[oi]
                K.dma(SP, og_t[oi], og_s[G * 512:(G + 1) * 512, hp * 128:(hp + 1) * 128].rearrange("(q p) d -> p q d", p=128),
                      reads=[allB[id(og_s)]], writes=[og_tB[oi]])
                bankset = [((0, 1), (2, 3)), ((4, 5), (6, 7))]
                gens = []
                for hl in range(2):
                    b = (hp % 2) * 2 + hl
                    gens.append(attention_gen(G, kt[b], ktBs[b], qt[b], qtBs[b], vaug[b], vBs[b], 0, 70, 64, pT, pTB, pcount, *bankset[hl]))
                run_chains(gens)
                for f_ in pending1:
                    f_()
                pending1.clear()
                for hl in range(2):
                    ob_ = bankset[hl][1]
                    Ov = ov_views(ob_, 64)
                    for qb in range(4):
                        c = stcol(1)
                        K.op(V, lambda e, qb=qb, c=c, Ov=Ov: e.reciprocal(out=st[:, c:c + 1], in_=Ov[qb // 2][:, qb % 2, 64:65]),
                             reads=[psB[ob_[qb // 2]]], writes=[stB[c]])
                        K.op(V, lambda e, qb=qb, c=c, hl=hl, Ov=Ov, obp=obp: e.tensor_scalar(out=obp[:, qb, hl * 64:(hl + 1) * 64], in0=Ov[qb // 2][:, qb % 2, 0:64],
                                                                                   scalar1=st[:, c:c + 1], scalar2=None, op0=ALU.mult),
                             reads=[psB[ob_[qb // 2]], stB[c]], writes=[obpB])
                K.op(V, lambda e, oi=oi, obp=obp: e.tensor_tensor(out=obp, in0=obp, in1=og_t[oi], op=ALU.mult), reads=[obpB, og_tB[oi]], writes=[obpB])

                def out1(oi=oi, obp=obp, obpB=obpB, hp=hp, G=G):
                    transposes(obp.rearrange("p a b -> p (a b)"), obpB, 4, 128, oT[oi].rearrange("p (a b) -> p a b", a=4), oTB[oi], bank=0)
                    K.dma(SP, attn1_s[hp][:, G * 512:(G + 1) * 512], oT[oi], reads=[oTB[oi]], writes=[allB[id(attn1_s)]])
                pending1.append(out1)
        for f_ in pending1:
            f_()
        pending1.clear()
        out_proj(attn1_s, allB[id(attn1_s)], fox_w_out_d[0], 8)

    def out_proj(attn_s, attnB, w_d, nfc):
        K.phase = f"outproj{nfc}"
        al = Bump()
        w_o = [al([128, nfc, 512], BF16) for _ in range(2)]; w_oB = [Buf("wo0"), Buf("wo1")]
        aT = [al([128, nfc, 512], BF16) for _ in range(2)]; aTB = [Buf("aT0"), Buf("aT1")]
        tmp = [al([128, 512], F32) for _ in range(2)]; tmpB = [Buf("tmpo0"), Buf("tmpo1")]
        for dh in range(2):
            K.dma(P_, w_o[dh], w_d.rearrange("(k p) d -> p k d", p=128)[:, :, dh * 512:(dh + 1) * 512], writes=[w_oB[dh]])
        cnt = 0
        for G in range(NG):
            a = aT[G % 2]; aB = aTB[G % 2]
            K.dma(SP, a, attn_s.ap().rearrange("f p s -> p f s")[:, :, G * 512:(G + 1) * 512], reads=[attnB], writes=[aB])
            for c in range(4):
                t = G * 4 + c
                for dh in range(2):
                    pi = 4 + cnt % 2; tb = cnt % 2; cnt += 1
                    for fc in range(nfc):
                        K.op(T, lambda e, pi=pi, fc=fc, c=c, dh=dh, a=a: e.matmul(ps[pi][:, :], lhsT=a[:, fc, c * 128:(c + 1) * 128],
                                                                                  rhs=w_o[dh][:, fc, :], start=(fc == 0), stop=(fc == nfc - 1)),
                             reads=[aB, w_oB[dh]], writes=[psB[pi]])
                    K.op(V, lambda e, pi=pi, tb=tb, dh=dh: e.tensor_tensor(out=tmp[tb], in0=ps[pi][:, :], in1=gate[:, dh * 512:(dh + 1) * 512],
                                                                           op=ALU.mult), reads=[psB[pi], gateB], writes=[tmpB[tb]])
                    K.op(P_, lambda e, t=t, tb=tb, dh=dh: e.tensor_tensor(out=x_sb[:, t, dh * 512:(dh + 1) * 512],
                                                                          in0=x_sb[:, t, dh * 512:(dh + 1) * 512], in1=tmp[tb], op=ALU.add),
                         reads=[tmpB[tb], xB[t]], writes=[xB[t]])

    def finish():
        for t in range(NT):
            K.dma(SP, y_d[t * 128:(t + 1) * 128, :], x_sb[:, t, :], reads=[xB[t]], is_out=True)

    if "ffn1" not in skip:
        ffn(0, 0, ffw["ff1_gate"], ffw["ff1_up"], ffw["ff1_down"])
    if stop == "ffn1":
        finish()
        return nc, K
    mixer0()
    if stop in ("mix0", "m0_inproj", "m0_cc", "m0_attn"):
        finish()
        return nc, K
    ffn(0, 2, ffw["ff2_gate"], ffw["ff2_up"], ffw["ff2_down"])
    if stop == "l0":
        finish()
        return nc, K
    ffn(1, 0, ffw["ff1_gate"], ffw["ff1_up"], ffw["ff1_down"])
    if stop == "ffn3":
        finish()
        return nc, K
    mixer1()
    if stop == "mix1":
        finish()
        return nc, K
    ffn(1, 2, ffw["ff2_gate"], ffw["ff2_up"], ffw["ff2_down"])
    finish()
    return nc, K


SCOPES = [False]


def emit_program(nc, K):
    for e in ENGS:
        for op in K.ops[e]:
            for d in op.deps:
                if d.kind == "c":
                    d.signal = True
    for e in ENGS:
        n = 0
        for op in K.ops[e]:
            if op.kind == "c" and op.signal:
                n += 1
                op.val = n
    from contextlib import ExitStack
    with ExitStack() as es:
        csem = {e: es.enter_context(nc.semaphore("c_" + e)) for e in ENGS}
        dsem = {e: [es.enter_context(nc.semaphore(f"d_{e}{i}")) for i in range(NS)] for e in ("sp", "pool")}
        ccsem = [es.enter_context(nc.semaphore(f"cc{i}")) for i in range(NS)]
        block = es.enter_context(nc.Block())

        def semof(d):
            if d.kind == "c":
                return csem[d.eng], d.val
            if d.kind == "d":
                return dsem[d.eng][d.semi], d.val
            return ccsem[d.semi], d.val

        def run(ename, eng):
            for op in K.ops[ename]:
                for d in op.deps:
                    s, v = semof(d)
                    eng.wait_ge(s, v)
                if SCOPES[0]:
                    with nc.named_scope(op.phase):
                        ins = op.fn(eng)
                else:
                    ins = op.fn(eng)
                if op.kind == "c":
                    if op.signal:
                        ins.then_inc(csem[ename], 1)
                elif op.kind == "d":
                    ins.then_inc(dsem[ename][op.semi], 16)
                else:
                    ins.then_inc(ccsem[op.semi])
            if ename == "sp":
                for o in K.out_ops:
                    s, v = semof(o)
                    eng.wait_ge(s, v)

        @block.tensor
        def _(e):
            run("pe", e)

        @block.scalar
        def _(e):
            run("act", e)

        @block.vector
        def _(e):
            run("dve", e)

        @block.gpsimd
        def _(e):
            run("pool", e)

        @block.sync
        def _(e):
            run("sp", e)
    return nc


def own_blocks(r, NT):
    return [2 * i + ((i % 2) if r == 0 else 1 - (i % 2)) for i in range(NT)]


def make_in_maps(inputs, NT=16, lite=()):
    S = 2 * NT * 128
    maps = []
    ident = np.eye(128, dtype=np.float32)
    inv_a = (10000.0 ** (-np.arange(0, 64, 2, dtype=np.float32) / 64)).astype(np.float32)[None, :]
    inv_b = (10000.0 ** (-np.arange(0, 32, 2, dtype=np.float32) / 32)).astype(np.float32)[None, :]
    pp = np.arange(128)
    causal = np.where(pp[:, None] <= pp[None, :], 0.0, NEG).astype(np.float32)
    vis = np.zeros((128, 128), np.float32)
    hid = np.full((128, 128), NEG, np.float32)
    for c in range(N_CORES):
        b, r = c // 2, c % 2
        blocks = own_blocks(r, NT)
        xb = np.asarray(inputs["x"])[b, :S].reshape(2 * NT, 128, D)[blocks].reshape(NT * 128, D)
        pos = np.asarray(inputs["positions"])[b, :S].reshape(2 * NT, 128)[blocks]
        m = np.zeros((128, 4, 128), np.float32)
        for kr in range(2):
            for par in range(2):
                if kr == r:
                    mm = causal
                else:
                    own_first = (par == 0) if r == 0 else (par == 1)
                    mm = hid if own_first else vis
                m[:, kr * 2 + par, :] = mm
        d = {
            "x": np.ascontiguousarray(xb, dtype=np.float32),
            "cT": np.ascontiguousarray(np.asarray(inputs["c"])[b].reshape(8, 128).T, dtype=np.float32),
            "pos": np.ascontiguousarray(pos.T, dtype=np.int32),
            "ident_bf": ident.astype(ml_dtypes.bfloat16),
            "ident_f": ident,
            "masks": m.astype(ml_dtypes.bfloat16),
            "inv_a": inv_a, "inv_b": inv_b,
            "sel": np.array([[1.0, 0.0]] if r == 0 else [[0.0, 1.0]], np.float32),
        }
        for k in ("ada_w", "ada_b", "ff1_gate", "ff1_up", "ff1_down", "ff2_gate", "ff2_up", "ff2_down", "ab_w_in",
                  "mla_w_qb", "mla_w_kvb", "mla_q_lat_g", "mla_kv_lat_g", "diff_q_g", "diff_k_g", "mla_q_g", "mla_k_g",
                  "diff_lam_q1", "diff_lam_k1", "diff_lam_q2", "diff_lam_k2", "diff_subln_g", "ab_w_out", "fox_w_in",
                  "fox_b_f", "fox_q_g", "fox_k_g", "fox_w_out"):
            if k in lite:
                d[k] = np.zeros([1] * np.asarray(inputs[k]).ndim, np.float32)
            else:
                d[k] = np.ascontiguousarray(np.asarray(inputs[k]), dtype=np.float32)
        maps.append(d)
    return maps


def assemble(results, NT=16):
    S = 2 * NT * 128
    out = np.zeros((4, S, D), np.float32)
    for c in range(N_CORES):
        b, r = c // 2, c % 2
        blocks = own_blocks(r, NT)
        y = np.asarray(results[c]["y"]).reshape(NT, 128, D)
        ov = out[b].reshape(2 * NT, 128, D)
        for i, g in enumerate(blocks):
            ov[g] = y[i]
    return out


_CACHE = {}


def kernel(**inputs):
    if "nc" not in _CACHE:
        nc, K = build_program(16)
        emit_program(nc, K)
        _CACHE["nc"] = nc
    nc = _CACHE["nc"]
    maps = make_in_maps(inputs, 16)
    res = run_bass_kernel_spmd(nc, maps, core_ids=list(range(N_CORES)))
    return assemble(res.results, 16)
```

```python
import math
import numpy as np
import ml_dtypes
import concourse.bass as bass
import concourse.mybir as mybir
from concourse.bass_utils import run_bass_kernel_spmd

F32 = mybir.dt.float32
BF16 = mybir.dt.bfloat16
I32 = mybir.dt.int32
AF = mybir.ActivationFunctionType
ALU = mybir.AluOpType
AX = mybir.AxisListType

D = 1024
DFF = 2816
NFC = DFF // 128
EPS = 1e-6
NEG = -30000.0
N_CORES = 8
ENGS = ["pe", "act", "dve", "pool", "sp"]
NS = 16


class Buf:
    __slots__ = ("name", "w", "r", "excl")

    def __init__(self, name, excl=False):
        self.name = name
        self.w = None
        self.r = []
        self.excl = excl


class Op:
    __slots__ = ("eng", "idx", "fn", "deps", "signal", "kind", "semi", "val", "phase")


class Sched:
    def __init__(self, nc):
        self.nc = nc
        self.ops = {e: [] for e in ENGS}
        self.dmas = {e: [] for e in ENGS}
        self.seen = {e: {} for e in ENGS}
        self.seen_dma = {e: set() for e in ENGS}
        self.ccs = []
        self.out_ops = []
        self.pending = {e: [] for e in ENGS}
        self.phase = "prologue"

    def barrier(self):
        deps = []
        for e in ENGS:
            comp = [o for o in self.ops[e] if o.kind == "c"]
            if comp:
                deps.append(comp[-1])
            deps.extend(self.dmas[e][-NS:])
        deps.extend(self.ccs[-NS:])
        for e in ENGS:
            self.pending[e] = list(deps)

    def _add(self, eng, fn, reads, writes, kind):
        op = Op()
        op.eng, op.fn, op.kind, op.signal, op.semi, op.val = eng, fn, kind, False, None, None
        op.phase = self.phase
        deps = []
        raw = set()
        for b in reads:
            if b.w is not None:
                deps.append(b.w)
                raw.add(id(b.w))
            if b.excl:
                deps.extend(r for r in b.r if r.eng != eng)
        for b in writes:
            if b.w is not None:
                deps.append(b.w)
            deps.extend(b.r)
        if self.pending[eng]:
            for d in self.pending[eng]:
                deps.append(d)
                raw.add(id(d))
            self.pending[eng] = []
        if kind == "d":
            k = len(self.dmas[eng])
            if k >= NS:
                deps.append(self.dmas[eng][k - NS])
        if kind == "cc":
            k = len(self.ccs)
            if k >= NS:
                deps.append(self.ccs[k - NS])
        final = []
        for d in deps:
            if d.kind == "c":
                if d.eng == eng and kind == "c":
                    if eng == "pe" or id(d) not in raw:
                        continue
                if self.seen[eng].get(d.eng, -1) >= d.idx:
                    continue
                self.seen[eng][d.eng] = d.idx
                final.append(d)
            else:
                if id(d) in self.seen_dma[eng]:
                    continue
                self.seen_dma[eng].add(id(d))
                final.append(d)
        op.deps = final
        op.idx = len(self.ops[eng])
        self.ops[eng].append(op)
        if kind == "d":
            k = len(self.dmas[eng])
            op.semi, op.val = k % NS, 16 * (k // NS + 1)
            self.dmas[eng].append(op)
        if kind == "cc":
            k = len(self.ccs)
            op.semi, op.val = k % NS, k // NS + 1
            self.ccs.append(op)
        for b in reads:
            if kind == "c":
                b.r = [o for o in b.r if not (o.kind == "c" and o.eng == eng)]
            b.r.append(op)
        for b in writes:
            b.w = op
            b.r = []
        return op

    def op(self, eng, fn, reads=(), writes=()):
        return self._add(eng, fn, list(reads), list(writes), "c")

    def dma(self, eng, out, in_, reads=(), writes=(), is_out=False):
        o = self._add(eng, lambda e: e.dma_start(out=out, in_=in_), list(reads), list(writes), "d")
        if is_out:
            self.out_ops.append(o)
        return o

    def cc(self, fn, reads=(), writes=()):
        return self._add("pool", fn, list(reads), list(writes), "cc")

    def emit(self):
        nc = self.nc
        for e in ENGS:
            for op in self.ops[e]:
                for d in op.deps:
                    if d.kind == "c":
                        d.signal = True
        for e in ENGS:
            n = 0
            for op in self.ops[e]:
                if op.kind == "c" and op.signal:
                    n += 1
                    op.val = n
        sems = {e: nc.alloc_semaphore("s_" + e) if hasattr(nc, "alloc_semaphore") else None for e in ENGS}
        return sems


def build_program(NT=16, stop=None, lite=(), skip=()):
    nc = bass.Bass("TRN2", target_bir_lowering=False)
    K = Sched(nc)
    S_OWN = NT * 128
    S_ALL = 2 * S_OWN
    CH = min(8, NT)
    NCHK = NT // CH
    NG = NT // 4
    CW = CH * 128

    def din(name, shape, dt=F32):
        if name in lite:
            shape = [1] * len(shape)
        return nc.dram_tensor(name, list(shape), dt, kind="ExternalInput").ap()

    x_d = din("x", [S_OWN, D])
    cT_d = din("cT", [128, 8])
    pos_d = din("pos", [128, NT], I32)
    ada_w_d = din("ada_w", [2, D, 9 * D])
    ada_b_d = din("ada_b", [2, 9 * D])
    ffw = {}
    for nm in ("ff1_gate", "ff1_up", "ff2_gate", "ff2_up"):
        ffw[nm] = din(nm, [2, D, DFF])
    for nm in ("ff1_down", "ff2_down"):
        ffw[nm] = din(nm, [2, DFF, D])
    ab_w_in_d = din("ab_w_in", [1, D, 3744])
    w_qb_d = din("mla_w_qb", [1, 384, 768])
    w_kvb_d = din("mla_w_kvb", [1, 256, 1536])
    q_lat_g_d = din("mla_q_lat_g", [1, 384])
    kv_lat_g_d = din("mla_kv_lat_g", [1, 256])
    diff_q_g_d = din("diff_q_g", [1, 64])
    diff_k_g_d = din("diff_k_g", [1, 64])
    mla_q_g_d = din("mla_q_g", [1, 96])
    mla_k_g_d = din("mla_k_g", [1, 96])
    lam_d = [din(n, [1, 64]) for n in ("diff_lam_q1", "diff_lam_k1", "diff_lam_q2", "diff_lam_k2")]
    subln_g_d = din("diff_subln_g", [1, 128])
    ab_w_out_d = din("ab_w_out", [1, 2048, D])
    fox_w_in_d = din("fox_w_in", [1, D, 4112])
    fox_b_f_d = din("fox_b_f", [1, 16])
    fox_q_g_d = din("fox_q_g", [1, 64])
    fox_k_g_d = din("fox_k_g", [1, 64])
    fox_w_out_d = din("fox_w_out", [1, D, D])
    ident_bf_d = din("ident_bf", [128, 128], BF16)
    ident_f_d = din("ident_f", [128, 128])
    masks_d = din("masks", [128, 4, 128], BF16)
    inv_a_d = din("inv_a", [1, 32])
    inv_b_d = din("inv_b", [1, 16])
    sel_d = din("sel", [1, 2])
    y_d = nc.dram_tensor("y", [S_OWN, D], F32, kind="ExternalOutput").ap()

    def dscr(name, shape, dt=BF16):
        return nc.dram_tensor(name, list(shape), dt)

    qtA_s = dscr("qtA", [8, 128, S_OWN])
    qtB_s = dscr("qtB", [8, 96, S_OWN])
    ktA_loc = [dscr(f"ktA_loc{g}", [4 * 128, S_OWN]) for g in range(2)]
    ktA_all = [dscr(f"ktA_all{g}", [2 * 4 * 128, S_OWN]) for g in range(2)]
    ktB_loc = [dscr(f"ktB_loc{g}", [4 * 96, S_OWN]) for g in range(2)]
    ktB_all = [dscr(f"ktB_all{g}", [2 * 4 * 96, S_OWN]) for g in range(2)]
    NVH = 2 if NT >= 2 else 1
    VR = S_OWN // NVH
    vA_loc = [dscr(f"vA_loc{g}", [VR, 1024]) for g in range(NVH)]
    vA_all = [dscr(f"vA_all{g}", [2 * VR, 1024]) for g in range(NVH)]
    vB_loc = [dscr(f"vB_loc{g}", [VR, 1024]) for g in range(NVH)]
    vB_all = [dscr(f"vB_all{g}", [2 * VR, 1024]) for g in range(NVH)]
    attn0_s = dscr("attn0", [16, 128, S_OWN])
    qtF_s = dscr("qtF", [16, 70, S_OWN])
    ktF_loc = [dscr(f"ktF_loc{g}", [4 * 70, S_OWN]) for g in range(4)]
    ktF_all = [dscr(f"ktF_all{g}", [2 * 4 * 70, S_OWN]) for g in range(4)]
    vF_loc = [dscr(f"vF_loc{g}", [VR, 1024]) for g in range(NVH)]
    vF_all = [dscr(f"vF_all{g}", [2 * VR, 1024]) for g in range(NVH)]
    lf_loc = dscr("lf_loc", [16, S_OWN], F32)
    lf_all = dscr("lf_all", [32, S_OWN], F32)
    attn1_s = dscr("attn1", [8, 128, S_OWN])
    og_s = dscr("og", [S_OWN, 1024])

    def sb(name, shape, dt=F32):
        return nc.alloc_sbuf_tensor(name, list(shape), dt)

    x_sb = sb("x_sb", [128, NT, D])
    xB = [Buf(f"x{t}") for t in range(NT)]
    ident_bf = sb("ident_bf_sb", [128, 128], BF16)
    ident_f = sb("ident_f_sb", [128, 128])
    masks = sb("masks_sb", [128, 4, 128], BF16)
    cB = Buf("consts")
    cosA = sb("cosA", [128, NT, 32]); sinA = sb("sinA", [128, NT, 32])
    cosB = sb("cosB", [128, NT, 16]); sinB = sb("sinB", [128, NT, 16])
    ropeB = Buf("rope")
    neghalf = sb("neghalf", [128, 16])
    cond = sb("cond", [128, 8])
    condB = Buf("cond")
    gate = sb("gate", [128, D])
    gateB = Buf("gate")
    modcol = sb("modcol", [128, 2, 8])
    modcolB = Buf("modcol")
    adab_row = sb("adab_row", [16, 128])
    adab_rowB = Buf("adab_row")
    xs_bf = sb("xs_bf", [128, D], BF16)
    xsB = Buf("xs")
    junk = sb("junk", [128, D], BF16)
    junkB = Buf("junk")
    st = sb("stats", [128, 256])
    stB = [Buf(f"st{i}") for i in range(256)]
    ARENA = 124 * 1024
    arena = sb("arena", [128, ARENA // 2], BF16)

    class Bump:
        def __init__(self):
            self.off = 0
            K.barrier()

        def __call__(self, shape, dt):
            esz = 2 if dt == BF16 else 4
            n = int(np.prod(shape[1:])) * esz
            n = (n + 63) // 64 * 64
            v = aview(self.off, shape, dt)
            self.off += n
            assert self.off <= ARENA, (self.off, ARENA)
            return v

    def aview(off, shape, dt):
        esz = 2 if dt == BF16 else 4
        n = int(np.prod(shape[1:]))
        v = arena[0:shape[0], off // 2: off // 2 + n * esz // 2]
        if dt != BF16:
            v = v.bitcast(dt)
        if len(shape) == 3:
            v = v.rearrange("p (a b) -> p a b", a=shape[1])
        return v

    ps = [nc.alloc_psum_tensor(f"ps{i}", [128, 512], F32) for i in range(8)]
    psB = [Buf(f"ps{i}", excl=True) for i in range(8)]

    def ps_bf(i):
        return ps[i][:, :].bitcast(BF16)

    sched = K
    V, A, P_, T, SP = "dve", "act", "pool", "pe", "sp"

    K.dma(SP, ident_bf[:, :], ident_bf_d, writes=[cB])
    K.dma(SP, ident_f[:, :], ident_f_d, writes=[cB])
    K.dma(SP, masks[:, :, :], masks_d, writes=[cB])
    for t in range(NT):
        K.dma(SP, x_sb[:, t, :], x_d[t * 128:(t + 1) * 128, :], writes=[xB[t]])
    K.dma(SP, cond[:, :], cT_d, writes=[condB])
    K.op(P_, lambda e: e.memset(neghalf[:, :], -0.5), writes=[cB])
    K.op(A, lambda e: e.activation(out=cond[:, :], in_=cond[:, :], func=AF.Silu), reads=[condB], writes=[condB])

    pos_i = sb("pos_i", [128, NT], I32)
    pos_f = sb("pos_f", [128, NT])
    inv_a = sb("inv_a_sb", [128, 32]); inv_b = sb("inv_b_sb", [128, 16])
    K.dma(SP, pos_i[:, :], pos_d, writes=[ropeB])
    K.dma(SP, inv_a[:, :], inv_a_d.partition_broadcast(128), writes=[ropeB])
    K.dma(SP, inv_b[:, :], inv_b_d.partition_broadcast(128), writes=[ropeB])
    K.op(V, lambda e: e.tensor_copy(out=pos_f[:, :], in_=pos_i[:, :]), reads=[ropeB], writes=[ropeB])
    TWO_PI = 2.0 * math.pi
    ri = aview(0, [128, NT, 32], F32).bitcast(I32)
    rf = aview(NT * 32 * 4, [128, NT, 32], F32)
    for (ct, stt, inv, nf) in ((cosA, sinA, inv_a, 32), (cosB, sinB, inv_b, 16)):
        K.op(V, lambda e, inv=inv: e.tensor_scalar(out=inv[:, :], in0=inv[:, :], scalar1=1.0 / TWO_PI, scalar2=None,
                                                   op0=ALU.mult), reads=[ropeB], writes=[ropeB])
        for (dst, shift) in ((stt, 0.5), (ct, 0.75)):
            for t in range(NT):
                K.op(V, lambda e, dst=dst, t=t, inv=inv, shift=shift: e.tensor_scalar(
                    out=dst[:, t, :], in0=inv[:, :], scalar1=pos_f[:, t:t + 1], scalar2=shift,
                    op0=ALU.mult, op1=ALU.add), reads=[ropeB], writes=[ropeB])
            K.op(V, lambda e, dst=dst, nf=nf: e.tensor_copy(out=ri[:, :, 0:nf], in_=dst[:, :, :]), reads=[ropeB], writes=[ropeB])
            K.op(V, lambda e, dst=dst, nf=nf: e.tensor_copy(out=rf[:, :, 0:nf], in_=ri[:, :, 0:nf]), reads=[ropeB], writes=[ropeB])
            K.op(V, lambda e, dst=dst, nf=nf: e.tensor_tensor(out=dst[:, :, :], in0=dst[:, :, :], in1=rf[:, :, 0:nf],
                                                              op=ALU.subtract), reads=[ropeB], writes=[ropeB])
            K.op(V, lambda e, dst=dst, nf=nf: e.tensor_scalar(out=rf[:, :, 0:nf], in0=dst[:, :, :], scalar1=0.0, scalar2=None,
                                                              op0=ALU.is_lt), reads=[ropeB], writes=[ropeB])
            K.op(V, lambda e, dst=dst, nf=nf: e.tensor_tensor(out=dst[:, :, :], in0=dst[:, :, :], in1=rf[:, :, 0:nf],
                                                              op=ALU.add), reads=[ropeB], writes=[ropeB])
            K.op(V, lambda e, dst=dst: e.tensor_scalar(out=dst[:, :, :], in0=dst[:, :, :], scalar1=TWO_PI, scalar2=-math.pi,
                                                       op0=ALU.mult, op1=ALU.add), reads=[ropeB], writes=[ropeB])
            K.op(V, lambda e, dst=dst: e.tensor_scalar(out=dst[:, :, :], in0=dst[:, :, :], scalar1=math.pi, scalar2=-math.pi,
                                                       op0=ALU.min, op1=ALU.max), reads=[ropeB], writes=[ropeB])
            K.op(A, lambda e, dst=dst: e.activation(out=dst[:, :, :], in_=dst[:, :, :], func=AF.Sin),
                 reads=[ropeB], writes=[ropeB])

    def compute_mod(l, sub, gate_mult):
        prev_phase = K.phase
        K.phase = f"mod_l{l}s{sub}"
        al = Bump()
        aw = [al([128, 8, 512], F32) for i in range(2)]
        awB = [Buf(f"aw{i}") for i in range(2)]
        cond_rep = al([128, 8, 128], F32)
        crB = Buf("cond_rep")
        K.op(V, lambda e: e.tensor_copy(out=cond_rep, in_=cond[:, :].unsqueeze(2).broadcast_to([128, 8, 128])),
             reads=[condB], writes=[crB])
        m_sh, m_sc, m_g = 3 * sub, 3 * sub + 1, 3 * sub + 2
        K.dma(SP, adab_row[:, :], ada_b_d[l, m_sh * D:(m_sh + 2) * D].rearrange("(r c) -> r c", c=128),
              writes=[adab_rowB])
        gb = al([128, D], F32)
        gbB = Buf("gb")
        K.dma(SP, gb, ada_b_d[l:l + 1, m_g * D:(m_g + 1) * D].partition_broadcast(128), writes=[gbB])
        bi = 0
        colps = ps[7]
        first_col = True
        for (m, kind) in ((m_sh, "col"), (m_sc, "col"), (m_g, "row")):
            for half in range(2):
                a = aw[bi % 2]; aB = awB[bi % 2]; bi += 1
                c0 = m * D + half * 512
                K.dma(SP, a, ada_w_d[l].rearrange("(k p) c -> p k c", p=128)[:, :, c0:c0 + 512], writes=[aB])
                if kind == "row":
                    pst = ps[half]
                    for k in range(8):
                        K.op(T, lambda e, a=a, k=k, pst=pst: e.matmul(pst[:, :], lhsT=cond_rep[:, k, :], rhs=a[:, k, :],
                                                                      start=(k == 0), stop=(k == 7)),
                             reads=[aB, crB], writes=[psB[half]])
                    K.op(V, lambda e, pst=pst, half=half: e.tensor_tensor(
                        out=gate[:, half * 512:(half + 1) * 512], in0=pst[:, :], in1=gb[:, half * 512:(half + 1) * 512],
                        op=ALU.add), reads=[psB[half], gbB], writes=[gateB])
                else:
                    which = 0 if m == m_sc else 1
                    for jb in range(4):
                        col = which * 8 + half * 4 + jb
                        for k in range(8):
                            K.op(T, lambda e, a=a, k=k, jb=jb, col=col: e.matmul(
                                colps[:, col:col + 1], lhsT=a[:, k, jb * 128:(jb + 1) * 128], rhs=cond[:, k:k + 1],
                                start=(k == 0), stop=(k == 7)), reads=[aB, condB], writes=[psB[7]])
        if gate_mult != 1.0:
            K.op(V, lambda e: e.tensor_scalar(out=gate[:, :], in0=gate[:, :], scalar1=gate_mult, scalar2=None,
                                              op0=ALU.mult), reads=[gateB], writes=[gateB])
        K.op(T, lambda e: e.transpose(ps[6][:, 0:16], adab_row[:, :], ident_f[0:16, 0:16]),
             reads=[adab_rowB, cB], writes=[psB[6]])
        K.op(V, lambda e: e.tensor_copy(out=st[:, 0:16], in_=ps[6][:, 0:16]), reads=[psB[6]], writes=[stB[0]])
        K.op(V, lambda e: e.scalar_tensor_tensor(out=modcol[:, 0, :], in0=colps[:, 0:8], scalar=1.0, in1=st[:, 8:16],
                                                 op0=ALU.add, op1=ALU.add), reads=[psB[7], stB[0]], writes=[modcolB])
        K.op(V, lambda e: e.tensor_tensor(out=modcol[:, 1, :], in0=colps[:, 8:16], in1=st[:, 0:8], op=ALU.add),
             reads=[psB[7], stB[0]], writes=[modcolB])
        K.phase = prev_phase

    trps = [6, 7]
    trcnt = [0]

    def mod_transpose(t, dstT, col0, dstB):
        xt = x_sb[:, t, :]
        K.op(V, lambda e: e.scalar_tensor_tensor(out=junk[:, :], in0=xt, scalar=1.0, in1=xt, op0=ALU.mult, op1=ALU.mult,
                                                 accum_out=st[:, 16:17]), reads=[xB[t]], writes=[junkB, stB[16]])
        K.op(V, lambda e: e.tensor_scalar(out=st[:, 17:18], in0=st[:, 16:17], scalar1=1.0 / D, scalar2=EPS,
                                          op0=ALU.mult, op1=ALU.add), reads=[stB[16]], writes=[stB[17]])
        K.op(P_, lambda e: e.tensor_tensor(out=st[:, 18:19], in0=st[:, 17:18], in1=neghalf[:, 0:1], op=ALU.pow),
             reads=[stB[17], cB], writes=[stB[18]])
        K.op(V, lambda e: e.tensor_scalar(out=xs_bf[:, :], in0=xt, scalar1=st[:, 18:19], scalar2=None, op0=ALU.mult),
             reads=[xB[t], stB[18]], writes=[xsB])
        pi = trps[trcnt[0] % 2]; trcnt[0] += 1
        pv = ps_bf(pi)
        for k in range(8):
            K.op(T, lambda e, k=k: e.transpose(pv[:, k * 128:(k + 1) * 128], xs_bf[:, k * 128:(k + 1) * 128], ident_bf[:, :]),
                 reads=[xsB, cB], writes=[psB[pi]])
        for k in range(8):
            K.op(V, lambda e, k=k: e.tensor_scalar(out=dstT[:, k, col0:col0 + 128], in0=pv[:, k * 128:(k + 1) * 128],
                                                   scalar1=modcol[:, 0, k:k + 1], scalar2=modcol[:, 1, k:k + 1],
                                                   op0=ALU.mult, op1=ALU.add),
                 reads=[psB[pi], modcolB], writes=[dstB])

    def ffn(l, sub, wg_d, wu_d, wd_d):
        K.phase = f"ffn_l{l}s{sub}"
        compute_mod(l, sub, 0.5)
        al = Bump()
        xnT = al([128, 8, CW], BF16)
        hT = al([128, NFC, CW], BF16)
        wd = [al([128, NFC, 512], BF16) for i in range(2)]
        wgu = [[al([128, 8, 256], BF16) for j in range(2)] for i in range(2)]
        tmp = [al([128, 512], F32) for i in range(2)]
        xnB = [Buf(f"xnT{c}") for c in range(CH)]
        hB = [[Buf(f"hT{fc}_{tg}") for tg in range(CW // 512 if CW >= 512 else 1)] for fc in range(NFC)]
        wdB = [Buf("wd0"), Buf("wd1")]
        wguB = [Buf("wgu0"), Buf("wgu1")]
        tmpB = [Buf("tmp0"), Buf("tmp1")]
        TG = max(1, CW // 512)
        TW = min(512, CW)
        for ck in range(NCHK):
            for c in range(CH):
                mod_transpose(ck * CH + c, xnT, c * 128, xnB[c])
            cnt = 0
            for j in range(NFC // 2):
                wb = j % 2
                if j in (2, 4):
                    dh_ = (j - 2) // 2
                    K.dma(P_, wd[dh_], wd_d[l].rearrange("(k p) d -> p k d", p=128)[:, :, dh_ * 512:(dh_ + 1) * 512],
                          writes=[wdB[dh_]])
                K.dma(P_, wgu[wb][0], wg_d[l].rearrange("(k p) f -> p k f", p=128)[:, :, j * 256:(j + 1) * 256],
                      writes=[wguB[wb]])
                K.dma(P_, wgu[wb][1], wu_d[l].rearrange("(k p) f -> p k f", p=128)[:, :, j * 256:(j + 1) * 256],
                      writes=[wguB[wb]])
                for fl in range(2):
                    fc = 2 * j + fl
                    for tg in range(TG):
                        pa, pb = (cnt % 2) * 2, (cnt % 2) * 2 + 1
                        tb = cnt % 2
                        cnt += 1
                        rd = [xnB[c] for c in range(tg * 4, min(CH, tg * 4 + 4))] + [wguB[wb]]
                        for (pi, wi) in ((pa, 0), (pb, 1)):
                            for k in range(8):
                                K.op(T, lambda e, pi=pi, wi=wi, k=k, fl=fl, tg=tg, wb=wb: e.matmul(
                                    ps[pi][:, 0:TW], lhsT=wgu[wb][wi][:, k, fl * 128:(fl + 1) * 128],
                                    rhs=xnT[:, k, tg * 512:tg * 512 + TW], start=(k == 0), stop=(k == 7)),
                                     reads=rd, writes=[psB[pi]])
                        K.op(A, lambda e, pa=pa, tb=tb: e.activation(out=tmp[tb][:, 0:TW], in_=ps[pa][:, 0:TW], func=AF.Silu),
                             reads=[psB[pa]], writes=[tmpB[tb]])
                        K.op(V, lambda e, pb=pb, tb=tb, fc=fc, tg=tg: e.tensor_tensor(
                            out=hT[:, fc, tg * 512:tg * 512 + TW], in0=ps[pb][:, 0:TW], in1=tmp[tb][:, 0:TW], op=ALU.mult),
                             reads=[psB[pb], tmpB[tb]], writes=[hB[fc][tg]])
            for dh in range(2):
                for c in range(CH):
                    t = ck * CH + c
                    pi = 4 + (c % 2)
                    tg = c // 4
                    for fc in range(NFC):
                        K.op(T, lambda e, pi=pi, fc=fc, c=c, dh=dh: e.matmul(
                            ps[pi][:, :], lhsT=hT[:, fc, c * 128:(c + 1) * 128], rhs=wd[dh][:, fc, :],
                            start=(fc == 0), stop=(fc == NFC - 1)), reads=[hB[fc][tg], wdB[dh]], writes=[psB[pi]])
                    tb = c % 2
                    K.op(V, lambda e, pi=pi, tb=tb, dh=dh: e.tensor_tensor(
                        out=tmp[tb][:, :], in0=ps[pi][:, :], in1=gate[:, dh * 512:(dh + 1) * 512], op=ALU.mult),
                         reads=[psB[pi], gateB], writes=[tmpB[tb]])
                    K.op(P_, lambda e, t=t, tb=tb, dh=dh: e.tensor_tensor(
                        out=x_sb[:, t, dh * 512:(dh + 1) * 512], in0=x_sb[:, t, dh * 512:(dh + 1) * 512], in1=tmp[tb][:, :],
                        op=ALU.add), reads=[tmpB[tb], xB[t]], writes=[xB[t]])


    def bcast_load(dst, src_row, B):
        K.dma(SP, dst, src_row.partition_broadcast(128), writes=[B])

    stc = [0, 0]

    def stcol(n=1):
        if n > 4:
            c = 64 + 24 * (stc[0] % 4)
            stc[0] += 1
        else:
            c = 160 + 4 * (stc[1] % 24)
            stc[1] += 1
        return c

    def norm_rope_gen(src, srcB, ng, gw, g_tile, gB, out_bf, outB, W3, rope=None):
        sq, t1, t2, wB = W3
        n = ng * gw
        v3 = lambda ap: ap.rearrange("p (g w) -> p g w", g=ng)
        c = stcol(3 * 8)
        ssq, vv, rs = st[:, c:c + ng], st[:, c + 8:c + 8 + ng], st[:, c + 16:c + 16 + ng]
        sB = stB[c]
        K.op(A, lambda e: e.activation(out=sq[:, 0:n], in_=src, func=AF.Square), reads=[srcB], writes=[wB[0]])
        yield
        K.op(V, lambda e: e.tensor_reduce(out=ssq, in_=v3(sq[:, 0:n]), axis=AX.X, op=ALU.add), reads=[wB[0]], writes=[sB])
        yield
        K.op(V, lambda e: e.tensor_scalar(out=vv, in0=ssq, scalar1=1.0 / gw, scalar2=EPS, op0=ALU.mult, op1=ALU.add),
             reads=[sB], writes=[sB])
        yield
        K.op(P_, lambda e: e.tensor_tensor(out=rs, in0=vv, in1=neghalf[:, 0:ng], op=ALU.pow), reads=[sB, cB], writes=[sB])
        yield
        K.op(V, lambda e: e.tensor_tensor(out=v3(t1[:, 0:n]), in0=v3(src), in1=rs.unsqueeze(2).broadcast_to([128, ng, gw]),
                                          op=ALU.mult), reads=[srcB, sB], writes=[wB[1]])
        yield
        gb3 = g_tile.unsqueeze(1).broadcast_to([128, ng, gw])
        if rope is None:
            K.op(P_, lambda e: e.tensor_tensor(out=v3(out_bf), in0=v3(t1[:, 0:n]), in1=gb3, op=ALU.mult),
                 reads=[wB[1], gB], writes=[outB])
            yield
            return
        r0, rh, cos_t, sin_t = rope
        K.op(P_, lambda e: e.tensor_tensor(out=v3(t1[:, 0:n]), in0=v3(t1[:, 0:n]), in1=gb3, op=ALU.mult),
             reads=[wB[1], gB], writes=[wB[1]])
        yield
        t13, t23, o3 = v3(t1[:, 0:n]), v3(t2[:, 0:n]), v3(out_bf)
        if r0 > 0:
            K.op(A, lambda e: e.activation(out=o3[:, :, 0:r0], in_=t13[:, :, 0:r0], func=AF.Copy), reads=[wB[1]], writes=[outB])
            yield
        cb = cos_t.unsqueeze(1).broadcast_to([128, ng, rh])
        sb_ = sin_t.unsqueeze(1).broadcast_to([128, ng, rh])
        x1, x2 = t13[:, :, r0:r0 + rh], t13[:, :, r0 + rh:r0 + 2 * rh]
        a_, b_ = t23[:, :, 0:rh], t23[:, :, rh:2 * rh]
        K.op(V, lambda e: e.tensor_tensor(out=a_, in0=x1, in1=cb, op=ALU.mult), reads=[wB[1], ropeB], writes=[wB[2]])
        yield
        K.op(P_, lambda e: e.tensor_tensor(out=b_, in0=x2, in1=sb_, op=ALU.mult), reads=[wB[1], ropeB], writes=[wB[3]])
        yield
        K.op(V, lambda e: e.tensor_tensor(out=o3[:, :, r0:r0 + rh], in0=a_, in1=b_, op=ALU.subtract),
             reads=[wB[2], wB[3]], writes=[outB])
        yield
        K.op(V, lambda e: e.tensor_tensor(out=a_, in0=x2, in1=cb, op=ALU.mult), reads=[wB[1], ropeB], writes=[wB[2]])
        yield
        K.op(P_, lambda e: e.tensor_tensor(out=b_, in0=x1, in1=sb_, op=ALU.mult), reads=[wB[1], ropeB], writes=[wB[3]])
        yield
        K.op(V, lambda e: e.tensor_tensor(out=o3[:, :, r0 + rh:r0 + 2 * rh], in0=a_, in1=b_, op=ALU.add),
             reads=[wB[2], wB[3]], writes=[outB])
        yield

    def run_chains(gens):
        gens = list(gens)
        while gens:
            for g_ in list(gens):
                try:
                    next(g_)
                except StopIteration:
                    gens.remove(g_)

    def norm_rope(src, srcB, ng, gw, g_tile, gB, out_bf, outB, W3, rope=None):
        run_chains([norm_rope_gen(src, srcB, ng, gw, g_tile, gB, out_bf, outB, W3, rope)])

    def transposes(src_bf, srcB, n, w, dst, dstB, rows=128, eng=A, bank=None):
        if bank is None:
            pi = trps[trcnt[0] % 2]; trcnt[0] += 1
        else:
            pi = bank
        pv = ps_bf(pi)
        for i in range(n):
            K.op(T, lambda e, i=i: e.transpose(pv[0:w, i * 128:i * 128 + rows], src_bf[0:rows, i * w:(i + 1) * w],
                                               ident_bf[0:rows, 0:rows]), reads=[srcB, cB], writes=[psB[pi]])
        src3 = pv[0:w, 0:n * 128].rearrange("p (a b) -> p a b", a=n)[:, :, 0:rows]
        if eng == A:
            K.op(A, lambda e: e.activation(out=dst, in_=src3, func=AF.Copy), reads=[psB[pi]], writes=[dstB])
        else:
            K.op(V, lambda e: e.tensor_copy(out=dst, in_=src3), reads=[psB[pi]], writes=[dstB])

    def inproj_block(hT, hB, w_sb, wB_, c0, wdt, pi):
        for k in range(8):
            K.op(T, lambda e, k=k: e.matmul(ps[pi][:, 0:wdt], lhsT=hT[:, k, :], rhs=w_sb[:, k, c0:c0 + wdt],
                                            start=(k == 0), stop=(k == 7)), reads=[hB, wB_], writes=[psB[pi]])

    def allgather(loc, locB, allt, allB):
        K.cc(lambda e: e.collective_compute("AllGather", ALU.bypass, replica_groups=[[0, 1], [2, 3], [4, 5], [6, 7]],
                                            ins=[loc.ap().opt()], outs=[allt.ap().opt()]), reads=[locB], writes=[allB])

    def ov_views(obanks, dv):
        dv1 = dv + 1
        return [ps[obanks[0]][:, 0:2 * dv1].rearrange("p (a b) -> p a b", a=2),
                ps[obanks[1]][:, 0:2 * dv1].rearrange("p (a b) -> p a b", a=2)]

    def attention_gen(G, kt, ktB_, qt, qtB_, vaug, vB_, r0, nr, dv, pT, pTB, pcount, sbanks=(0, 1), obanks=(2, 3)):
        dv1 = dv + 1
        blocks = [(kr, j) for kr in range(2) for j in range(4 * G + 4)]
        Ov = ov_views(obanks, dv)
        pend = None
        started = [False, False]

        def do_pv(item):
            n, kr, j, c0, pb = item
            for qb in range(c0 // 128, 4):
                bank = qb // 2
                st_ = not started[bank]
                started[bank] = True
                last = (kr == 1 and j == 4 * G + qb)
                K.op(T, lambda e, qb=qb, bank=bank, st_=st_, last=last, pb=pb, kr=kr, j=j: e.matmul(
                    Ov[bank][:, qb % 2, :], lhsT=pT[pb][:, qb * 128:(qb + 1) * 128], rhs=vaug[:, kr * NT + j, 0:dv1],
                    start=st_, stop=last, skip_group_check=True), reads=[pTB[pb], vB_], writes=[psB[obanks[bank]]])

        for n, (kr, j) in enumerate(blocks):
            jl = j - 4 * G
            c0 = max(0, jl) * 128
            s_ = sbanks[n % 2]
            diag = jl >= 0
            K.op(T, lambda e, s_=s_, c0=c0, kr=kr, j=j, diag=diag: e.matmul(
                ps[s_][:, c0:512], lhsT=kt[r0:r0 + nr, kr, j * 128:(j + 1) * 128],
                rhs=qt[r0:r0 + nr, G * 512 + c0:G * 512 + 512], start=True, stop=not diag, skip_group_check=True),
                 reads=[ktB_, qtB_], writes=[psB[s_]])
            if diag:
                K.op(T, lambda e, s_=s_, c0=c0, kr=kr, jl=jl: e.matmul(
                    ps[s_][:, c0:c0 + 128], lhsT=ident_bf[:, :], rhs=masks[:, kr * 2 + (jl % 2), :], start=False, stop=True,
                    skip_group_check=True), reads=[cB], writes=[psB[s_]])
            pb = pcount[0] % len(pT); pcount[0] += 1
            K.op(A, lambda e, s_=s_, c0=c0, pb=pb: e.activation(out=pT[pb][:, c0:512], in_=ps[s_][:, c0:512], func=AF.Exp),
                 reads=[psB[s_]], writes=[pTB[pb]])
            if pend is not None:
                do_pv(pend)
            pend = (n, kr, j, c0, pb)
            yield
        do_pv(pend)

    def attention_map(G, kt, ktB_, qt, qtB_, vaug, vB_, r0, nr, dv, pT, pTB, pcount):
        run_chains([attention_gen(G, kt, ktB_, qt, qtB_, vaug, vB_, r0, nr, dv, pT, pTB, pcount)])
        return ov_views((2, 3), dv)

    def load_kt(kt, ktB_, all_list, hh, g, rows):
        for kr in range(2):
            K.dma(SP, kt[0:rows, kr, :], all_list[g][(kr * (all_list[g].shape[0] // (2 * rows)) + hh) * rows:
                                                     (kr * (all_list[g].shape[0] // (2 * rows)) + hh + 1) * rows, :],
                  reads=[allB[id(all_list[g])]], writes=[ktB_])

    def load_v(vaug, vB_, v_all, c0, dv):
        npv = NT // NVH
        for vg in range(NVH):
            for kr in range(2):
                K.dma(SP, vaug[:, kr * NT + vg * npv:kr * NT + (vg + 1) * npv, 0:dv],
                      v_all[vg][kr * VR:(kr + 1) * VR, c0:c0 + dv].rearrange("(j p) d -> p j d", p=128),
                      reads=[allB[id(v_all[vg])]], writes=[vB_])

    allB = {}

    def regB(t):
        allB[id(t)] = Buf("dram")
        return allB[id(t)]

    def mixer0():
        li = 0.8 - 0.6 * math.exp(-0.3 * 0)
        K.phase = "m0_inproj"
        compute_mod(0, 1, 1.0)
        al = Bump()
        w_in = al([128, 8, 3744], BF16); w_inB = Buf("w_in")
        w_qb = al([128, 3, 768], BF16); w_kvb = al([128, 2, 1536], BF16); wsB = Buf("w_small")
        hTt = [al([128, 8, 128], BF16) for _ in range(2)]; hTB = [Buf("hT0"), Buf("hT1")]
        gq = al([128, 64], F32); gk = al([128, 64], F32); gqb = al([128, 96], F32); gkb = al([128, 96], F32)
        gql = al([128, 384], F32); gkvl = al([128, 256], F32); gB = Buf("gains")
        W3s = [(al([128, 512], F32), al([128, 512], F32), al([128, 512], F32), [Buf(f"sq{i}"), Buf(f"t1{i}"), Buf(f"t2a{i}"), Buf(f"t2b{i}")])
               for i in range(4)]
        W3 = W3s[0]
        qk_bf = [al([128, 512], BF16) for _ in range(4)]; qkB = [Buf(f"qk{i}") for i in range(4)]
        qtA_t = al([128, 8, 128], BF16); ktA_t = al([128, 8, 128], BF16)
        qtB_t = al([128, 8, 128], BF16); ktB_t = al([128, 8, 128], BF16)
        stgB = [Buf("qtA_t"), Buf("ktA_t"), Buf("qtB_t"), Buf("ktB_t")]
        va_bf = al([128, 1024], BF16); vb_bf = al([128, 1024], BF16); vaB = Buf("va"); vbB = Buf("vb")
        cq_bf = al([128, 384], BF16); ckv_bf = al([128, 256], BF16); cqB = Buf("cq"); ckvB = Buf("ckv")
        cqT = al([128, 3, 128], BF16); ckvT = al([128, 2, 128], BF16); cqTB = Buf("cqT"); ckvTB = Buf("ckvT")
        kb_f = al([128, 768], F32); kbB = Buf("kb_f")
        for (dst, src) in ((gq, diff_q_g_d[0:1, :]), (gk, diff_k_g_d[0:1, :]), (gqb, mla_q_g_d[0:1, :]), (gkb, mla_k_g_d[0:1, :]),
                           (gql, q_lat_g_d[0:1, :]), (gkvl, kv_lat_g_d[0:1, :])):
            bcast_load(dst, src, gB)
        K.op(V, lambda e: e.tensor_scalar(out=gq, in0=gq, scalar1=64 ** -0.5, scalar2=None, op0=ALU.mult), reads=[gB], writes=[gB])
        K.op(V, lambda e: e.tensor_scalar(out=gqb, in0=gqb, scalar1=96 ** -0.5, scalar2=None, op0=ALU.mult), reads=[gB], writes=[gB])
        for c0 in range(0, 3744, 468):
            K.dma(P_, w_in[:, :, c0:c0 + 468], ab_w_in_d[0].rearrange("(k p) c -> p k c", p=128)[:, :, c0:c0 + 468], writes=[w_inB])
        K.dma(P_, w_qb, w_qb_d[0].rearrange("(k p) c -> p k c", p=128), writes=[wsB])
        K.dma(P_, w_kvb, w_kvb_d[0].rearrange("(k p) c -> p k c", p=128), writes=[wsB])
        for tl in (qtA_s, qtB_s, attn0_s) + tuple(ktA_loc + ktA_all + ktB_loc + ktB_all + vA_loc + vA_all + vB_loc + vB_all):
            regB(tl)
        npv = NT // NVH
        pcnt = 0
        import os as _os
        SEC = _os.environ.get("MIX0_SEC", "abcd")
        for t in range(NT):
            hT = hTt[t % 2]; hB_ = hTB[t % 2]
            mod_transpose(t, hT, 0, hB_)
            tok = slice(t * 128, (t + 1) * 128)
            specs = ((0, gq, qtA_t, stgB[0], True), (1024, gk, ktA_t, stgB[1], False))
            chains = []
            for si_, (base, g_t, st_t, stb, is_q) in enumerate(specs):
                for blk in range(2):
                    ci = si_ * 2 + blk
                    pi = ci
                    inproj_block(hT, hB_, w_in, w_inB, base + blk * 512, 512, pi)
                    chains.append(norm_rope_gen(ps[pi][:, 0:512], psB[pi], 8, 64, g_t, gB, qk_bf[ci], qkB[ci], W3s[ci],
                                                rope=(0, 32, cosA[:, t, :], sinA[:, t, :])))
            run_chains(chains)
            pcnt = 0
            for si_, (base, g_t, st_t, stb, is_q) in enumerate(specs):
                for blk in range(2):
                    ci = si_ * 2 + blk
                    transposes(qk_bf[ci], qkB[ci], 4, 128, st_t[:, blk * 4:(blk + 1) * 4, :], stb)
                if is_q:
                    K.dma(SP, qtA_s.ap().rearrange("h p s -> p h s")[:, :, tok], st_t, reads=[stb], writes=[allB[id(qtA_s)]])
                else:
                    for g in range(2):
                        K.dma(SP, ktA_loc[g].ap().rearrange("(h p) s -> p h s", p=128)[:, :, tok], st_t[:, g * 4:(g + 1) * 4, :],
                              reads=[stb], writes=[allB[id(ktA_loc[g])]])
            for blk in (range(2) if "b" in SEC else ()):
                pi = pcnt % 4; pcnt += 1
                inproj_block(hT, hB_, w_in, w_inB, 2048 + blk * 512, 512, pi)
                K.op(A, lambda e, pi=pi, blk=blk: e.activation(out=va_bf[:, blk * 512:(blk + 1) * 512], in_=ps[pi][:, :], func=AF.Copy),
                     reads=[psB[pi]], writes=[vaB])
            if "b" in SEC:
                K.dma(SP, vA_loc[t // npv][(t % npv) * 128:(t % npv + 1) * 128, :], va_bf, reads=[vaB], writes=[allB[id(vA_loc[t // npv])]])
            if "c" not in SEC and "d" not in SEC:
                continue
            pq = pcnt % 4; pcnt += 1
            inproj_block(hT, hB_, w_in, w_inB, 3072, 384, pq)
            pk = pcnt % 4; pcnt += 1
            inproj_block(hT, hB_, w_in, w_inB, 3456, 288, pk)
            run_chains([norm_rope_gen(ps[pq][:, 0:384], psB[pq], 1, 384, gql, gB, cq_bf, cqB, W3s[0]),
                        norm_rope_gen(ps[pk][:, 0:256], psB[pk], 1, 256, gkvl, gB, ckv_bf, ckvB, W3s[1])])
            kb3 = kb_f.rearrange("p (h w) -> p h w", h=8)
            K.op(V, lambda e, pk=pk: e.tensor_copy(out=kb3[:, :, 64:96], in_=ps[pk][:, 256:288].unsqueeze(1).broadcast_to([128, 8, 32])),
                 reads=[psB[pk]], writes=[kbB])
            transposes(cq_bf, cqB, 3, 128, cqT, cqTB)
            transposes(ckv_bf, ckvB, 2, 128, ckvT, ckvTB)
            for blk in range(4):
                pi = pcnt % 4; pcnt += 1
                for k in range(2):
                    K.op(T, lambda e, k=k, pi=pi, blk=blk: e.matmul(ps[pi][:, 0:384], lhsT=ckvT[:, k, :], rhs=w_kvb[:, k, blk * 384:(blk + 1) * 384],
                                                                   start=(k == 0), stop=(k == 1)), reads=[ckvTB, wsB], writes=[psB[pi]])
                p3 = ps[pi][:, 0:384].rearrange("p (h w) -> p h w", h=2)
                K.op(V, lambda e, p3=p3, blk=blk: e.tensor_copy(out=kb3[:, blk * 2:blk * 2 + 2, 0:64], in_=p3[:, :, 0:64]),
                     reads=[psB[pi]], writes=[kbB])
                K.op(A, lambda e, p3=p3, blk=blk: e.activation(out=vb_bf[:, blk * 256:(blk + 1) * 256].rearrange("p (h w) -> p h w", h=2),
                                                               in_=p3[:, :, 64:192], func=AF.Copy), reads=[psB[pi]], writes=[vbB])
            chains = []
            pis = []
            for blk in range(2):
                pi = pcnt % 4; pcnt += 1
                pis.append(pi)
                for k in range(3):
                    K.op(T, lambda e, k=k, pi=pi, blk=blk: e.matmul(ps[pi][:, 0:384], lhsT=cqT[:, k, :], rhs=w_qb[:, k, blk * 384:(blk + 1) * 384],
                                                                   start=(k == 0), stop=(k == 2)), reads=[cqTB, wsB], writes=[psB[pi]])
                chains.append(norm_rope_gen(ps[pi][:, 0:384], psB[pi], 4, 96, gqb, gB, qk_bf[blk][:, 0:384], qkB[blk], W3s[blk],
                                            rope=(64, 16, cosB[:, t, :], sinB[:, t, :])))
            for blk in range(2):
                chains.append(norm_rope_gen(kb_f[:, blk * 384:(blk + 1) * 384], kbB, 4, 96, gkb, gB, qk_bf[2 + blk][:, 0:384], qkB[2 + blk], W3s[2 + blk],
                                            rope=(64, 16, cosB[:, t, :], sinB[:, t, :])))
            run_chains(chains)
            for blk in range(2):
                transposes(qk_bf[blk], qkB[blk], 4, 96, qtB_t[0:96, blk * 4:(blk + 1) * 4, :], stgB[2])
            K.dma(SP, qtB_s.ap().rearrange("h p s -> p h s")[:, :, tok], qtB_t[0:96, :, :], reads=[stgB[2]], writes=[allB[id(qtB_s)]])
            for blk in range(2):
                transposes(qk_bf[2 + blk], qkB[2 + blk], 4, 96, ktB_t[0:96, blk * 4:(blk + 1) * 4, :], stgB[3])
            for g in range(2):
                K.dma(SP, ktB_loc[g].ap().rearrange("(h p) s -> p h s", p=96)[:, :, tok], ktB_t[0:96, g * 4:(g + 1) * 4, :],
                      reads=[stgB[3]], writes=[allB[id(ktB_loc[g])]])
            K.dma(SP, vB_loc[t // npv][(t % npv) * 128:(t % npv + 1) * 128, :], vb_bf, reads=[vbB], writes=[allB[id(vB_loc[t // npv])]])
        if stop == "m0_inproj":
            return
        K.phase = "m0_exchange"
        for g in range(2):
            allgather(ktA_loc[g], allB[id(ktA_loc[g])], ktA_all[g], allB[id(ktA_all[g])])
        for g in range(NVH):
            allgather(vA_loc[g], allB[id(vA_loc[g])], vA_all[g], allB[id(vA_all[g])])
        for g in range(2):
            allgather(ktB_loc[g], allB[id(ktB_loc[g])], ktB_all[g], allB[id(ktB_all[g])])
        for g in range(NVH):
            allgather(vB_loc[g], allB[id(vB_loc[g])], vB_all[g], allB[id(vB_all[g])])
        if stop == "m0_cc":
            return
        K.phase = "m0_attn"
        al = Bump()
        kt = [al([128, 2, S_OWN], BF16) for _ in range(2)]; ktBs = [Buf("kt0"), Buf("kt1")]
        qt = [al([128, S_OWN], BF16) for _ in range(2)]; qtBs = [Buf("qt0"), Buf("qt1")]
        vaug = [al([128, 2 * NT, 129], BF16) for _ in range(2)]; vBs = [Buf("v0"), Buf("v1")]
        pT = [al([128, 512], BF16) for _ in range(8)]; pTB = [Buf(f"pT{i}") for i in range(8)]
        on = [al([128, 4, 128], F32) for _ in range(2)]; onB = [Buf("on0"), Buf("on1")]
        o_f = al([128, 128], F32); o_fB = Buf("o_f")
        ob4 = al([128, 512], BF16); ob4B = Buf("ob4")
        ob4b = al([128, 512], BF16); ob4bB = Buf("ob4b")
        oT = [al([128, 512], BF16) for _ in range(2)]; oTB = [Buf("oT0"), Buf("oT1")]
        lamv = al([128, 4, 64], F32); lamB = Buf("lam")
        sg = al([128, 128], F32); sgB = Buf("sg")
        for i in range(2):
            K.op(P_, lambda e, i=i: e.memset(vaug[i][:, :, 128:129], 1.0), writes=[vBs[i]])
        for i in range(4):
            bcast_load(lamv[:, i, :], lam_d[i][0:1, :], lamB)
        bcast_load(sg, subln_g_d[0:1, :], sgB)
        K.op(V, lambda e: e.tensor_scalar(out=sg, in0=sg, scalar1=1.0 - li, scalar2=None, op0=ALU.mult), reads=[sgB], writes=[sgB])
        LC = 8
        for i in range(2):
            K.op(V, lambda e, i=i: e.scalar_tensor_tensor(out=junk[:, 0:64], in0=lamv[:, 2 * i, :], scalar=1.0, in1=lamv[:, 2 * i + 1, :],
                                                          op0=ALU.mult, op1=ALU.mult, accum_out=st[:, 60 + i:61 + i]),
                 reads=[lamB], writes=[junkB, stB[60]])
        K.op(A, lambda e: e.activation(out=st[:, 60:62], in_=st[:, 60:62], func=AF.Exp), reads=[stB[60]], writes=[stB[60]])
        K.op(V, lambda e: e.scalar_tensor_tensor(out=st[:, 62:63], in0=st[:, 61:62], scalar=-li, in1=st[:, 60:61],
                                                 op0=ALU.add, op1=ALU.subtract), reads=[stB[60]], writes=[stB[60]])
        pcount = [0]
        ocnt = 0
        def load_head0(h):
            bi = h % 2
            g, hh = (h % 8) // 4, (h % 8) % 4
            if h < 8:
                load_kt(kt[bi], ktBs[bi], ktA_all, hh, g, 128)
                K.dma(SP, qt[bi][:, :], qtA_s[h], reads=[allB[id(qtA_s)]], writes=[qtBs[bi]])
                load_v(vaug[bi], vBs[bi], vA_all, h * 128, 128)
            else:
                load_kt(kt[bi], ktBs[bi], ktB_all, hh, g, 96)
                K.dma(SP, qt[bi][0:96, :], qtB_s[h - 8], reads=[allB[id(qtB_s)]], writes=[qtBs[bi]])
                load_v(vaug[bi], vBs[bi], vB_all, (h - 8) * 128, 128)

        pending0 = []
        ob4c = [(ob4, ob4B), (ob4b, ob4bB)]
        load_head0(0)
        for h in range(16):
            is_diff = h < 8
            bi = h % 2
            if h + 1 < 16:
                load_head0(h + 1)
            def post_diff(G, Ovs):
                for mi in range(2):
                    Ov = Ovs[mi]
                    ob_ = (2, 3) if mi == 0 else (6, 7)
                    for qb in range(4):
                        c = stcol(1)
                        K.op(V, lambda e, qb=qb, c=c, Ov=Ov: e.reciprocal(out=st[:, c:c + 1], in_=Ov[qb // 2][:, qb % 2, 128:129]),
                             reads=[psB[ob_[qb // 2]]], writes=[stB[c]])
                        K.op(V, lambda e, qb=qb, c=c, mi=mi, Ov=Ov: e.tensor_scalar(out=on[mi][:, qb, :], in0=Ov[qb // 2][:, qb % 2, 0:128],
                                                                                   scalar1=st[:, c:c + 1], scalar2=None, op0=ALU.mult),
                             reads=[psB[ob_[qb // 2]], stB[c]], writes=[onB[mi]])
                for qb in range(4):
                    c = stcol(3)
                    K.op(V, lambda e, qb=qb: e.scalar_tensor_tensor(out=o_f, in0=on[1][:, qb, :], scalar=st[:, 62:63], in1=on[0][:, qb, :],
                                                                    op0=ALU.mult, op1=ALU.add), reads=[onB[0], onB[1], stB[60]], writes=[o_fB])
                    K.op(V, lambda e, c=c: e.scalar_tensor_tensor(out=junk[:, 0:128], in0=o_f, scalar=1.0, in1=o_f, op0=ALU.mult, op1=ALU.mult,
                                                                  accum_out=st[:, c:c + 1]), reads=[o_fB], writes=[junkB, stB[c]])
                    K.op(V, lambda e, c=c: e.tensor_scalar(out=st[:, c + 1:c + 2], in0=st[:, c:c + 1], scalar1=1.0 / 128, scalar2=EPS,
                                                           op0=ALU.mult, op1=ALU.add), reads=[stB[c]], writes=[stB[c]])
                    K.op(P_, lambda e, c=c: e.tensor_tensor(out=st[:, c + 2:c + 3], in0=st[:, c + 1:c + 2], in1=neghalf[:, 0:1], op=ALU.pow),
                         reads=[stB[c], cB], writes=[stB[c]])
                    dst_, dstB_ = ob4c[G % 2]
                    K.op(V, lambda e, c=c, qb=qb, dst_=dst_: e.scalar_tensor_tensor(out=dst_[:, qb * 128:(qb + 1) * 128], in0=o_f, scalar=st[:, c + 2:c + 3],
                                                                                   in1=sg, op0=ALU.mult, op1=ALU.mult),
                         reads=[o_fB, stB[c], sgB], writes=[dstB_])

            def post_mla(Ov, ob_, dst, dstB):
                for qb in range(4):
                    c = stcol(1)
                    K.op(V, lambda e, qb=qb, c=c: e.reciprocal(out=st[:, c:c + 1], in_=Ov[qb // 2][:, qb % 2, 128:129]),
                         reads=[psB[ob_[qb // 2]]], writes=[stB[c]])
                    K.op(V, lambda e, qb=qb, c=c: e.tensor_scalar(out=dst[:, qb * 128:(qb + 1) * 128], in0=Ov[qb // 2][:, qb % 2, 0:128],
                                                                  scalar1=st[:, c:c + 1], scalar2=None, op0=ALU.mult),
                         reads=[psB[ob_[qb // 2]], stB[c]], writes=[dstB])

            def emit_out(G, src=None, srcB=None, h=h, bank=0):
                nonlocal ocnt
                src = ob4 if src is None else src
                srcB = ob4B if srcB is None else srcB
                oi = ocnt % 2; ocnt += 1
                transposes(src, srcB, 4, 128, oT[oi].rearrange("p (a b) -> p a b", a=4), oTB[oi], bank=bank)
                K.dma(SP, attn0_s[h][:, G * 512:(G + 1) * 512], oT[oi], reads=[oTB[oi]], writes=[allB[id(attn0_s)]])

            def flush():
                for f_ in pending0:
                    f_()
                pending0.clear()

            args = (kt[bi], ktBs[bi], qt[bi], qtBs[bi], vaug[bi], vBs[bi])
            if is_diff:
                for G in range(NG):
                    run_chains([attention_gen(G, *args, 0, 64, 128, pT, pTB, pcount, (0, 1), (2, 3)),
                                attention_gen(G, *args, 64, 64, 128, pT, pTB, pcount, (4, 5), (6, 7))])
                    flush()
                    post_diff(G, [ov_views((2, 3), 128), ov_views((6, 7), 128)])
                    pending0.append(lambda G=G, eo=emit_out: eo(G, ob4c[G % 2][0], ob4c[G % 2][1]))
            else:
                for G0 in range(0, NG, 2):
                    Gs = [G_ for G_ in (G0, G0 + 1) if G_ < NG]
                    bankset = [((0, 1), (2, 3)), ((4, 5), (6, 7))]
                    run_chains([attention_gen(G_, *args, 0, 96, 128, pT, pTB, pcount, *bankset[i]) for i, G_ in enumerate(Gs)])
                    flush()
                    obs = [(ob4, ob4B), (ob4b, ob4bB)]
                    for i, G_ in enumerate(Gs):
                        post_mla(ov_views(bankset[i][1], 128), bankset[i][1], *obs[i])
                    for i, G_ in enumerate(Gs):
                        pending0.append(lambda G_=G_, i=i, eo=emit_out, obs=obs: eo(G_, obs[i][0], obs[i][1], bank=4 * i))
        for f_ in pending0:
            f_()
        pending0.clear()
        out_proj(attn0_s, allB[id(attn0_s)], ab_w_out_d[0], 16)

    def mixer1():
        K.phase = "m1_inproj"
        compute_mod(1, 1, 1.0)
        al = Bump()
        w_in = al([128, 8, 4112], BF16); w_inB = Buf("w_in1")
        hTt = [al([128, 8, 128], BF16) for _ in range(2)]; hTB = [Buf("hT0"), Buf("hT1")]
        gq = al([128, 64], F32); gk = al([128, 64], F32); bf_t = al([128, 16], F32); gB = Buf("gains1")
        W3s = [(al([128, 512], F32), al([128, 512], F32), None, [Buf(f"sq{i}"), Buf(f"t1{i}"), Buf(f"t2a{i}"), Buf(f"t2b{i}")]) for i in range(4)]
        qk_bf = [al([128, 512], BF16) for _ in range(4)]; qkB = [Buf(f"qk{i}") for i in range(4)]
        qt_t = al([128, 16, 128], BF16); kt_t = al([128, 16, 128], BF16); stgB = [Buf("qt_t"), Buf("kt_t")]
        v_bf = al([128, 1024], BF16); og_bf = al([128, 1024], BF16); vB1 = Buf("v1"); ogB = Buf("og")
        lf_tok = al([128, NT, 16], F32); lfB = Buf("lf_tok")
        zt = al([128, 16], F32); ztB = Buf("zt")
        bcast_load(gq, fox_q_g_d[0:1, :], gB)
        bcast_load(gk, fox_k_g_d[0:1, :], gB)
        bcast_load(bf_t, fox_b_f_d[0:1, :], gB)
        K.op(V, lambda e: e.tensor_scalar(out=gq, in0=gq, scalar1=64 ** -0.5, scalar2=None, op0=ALU.mult), reads=[gB], writes=[gB])
        for c0 in range(0, 4112, 514):
            K.dma(P_, w_in[:, :, c0:c0 + 514], fox_w_in_d[0].rearrange("(k p) c -> p k c", p=128)[:, :, c0:c0 + 514], writes=[w_inB])
        for tl in (qtF_s, attn1_s, og_s, lf_loc, lf_all) + tuple(ktF_loc + ktF_all + vF_loc + vF_all):
            regB(tl)
        K.op(P_, lambda e: e.memset(qt_t[64:70, :, :], 1.0), writes=[stgB[0]])
        K.op(P_, lambda e: e.memset(kt_t[64:70, :, :], 1.0), writes=[stgB[1]])
        npv = NT // NVH
        pcnt = 0
        for t in range(NT):
            hT = hTt[t % 2]; hB_ = hTB[t % 2]
            mod_transpose(t, hT, 0, hB_)
            tok = slice(t * 128, (t + 1) * 128)
            chains = []
            for ci, (base, g_t, blk) in enumerate(((0, gq, 0), (0, gq, 1), (1024, gk, 0), (1024, gk, 1))):
                pi = pcnt % 4; pcnt += 1
                inproj_block(hT, hB_, w_in, w_inB, base + blk * 512, 512, pi)
                chains.append(norm_rope_gen(ps[pi][:, 0:512], psB[pi], 8, 64, g_t, gB, qk_bf[ci], qkB[ci], W3s[ci]))
            run_chains(chains)
            for (base, g_t, st_t, stb, is_q) in ((0, gq, qt_t, stgB[0], True), (1024, gk, kt_t, stgB[1], False)):
                for blk in range(2):
                    ci = (0 if is_q else 2) + blk
                    transposes(qk_bf[ci], qkB[ci], 8, 64, st_t[0:64, blk * 8:(blk + 1) * 8, :], stb)
                if is_q:
                    K.dma(SP, qtF_s.ap().rearrange("h p s -> p h s")[:, :, tok], st_t[0:70, :, :], reads=[stb], writes=[allB[id(qtF_s)]])
                else:
                    for g in range(4):
                        K.dma(SP, ktF_loc[g].ap().rearrange("(h p) s -> p h s", p=70)[:, :, tok], st_t[0:70, g * 4:(g + 1) * 4, :],
                              reads=[stb], writes=[allB[id(ktF_loc[g])]])
            for blk in range(2):
                pi = pcnt % 4; pcnt += 1
                inproj_block(hT, hB_, w_in, w_inB, 2048 + blk * 512, 512, pi)
                K.op(A, lambda e, pi=pi, blk=blk: e.activation(out=v_bf[:, blk * 512:(blk + 1) * 512], in_=ps[pi][:, :], func=AF.Copy),
                     reads=[psB[pi]], writes=[vB1])
            K.dma(SP, vF_loc[t // npv][(t % npv) * 128:(t % npv + 1) * 128, :], v_bf, reads=[vB1], writes=[allB[id(vF_loc[t // npv])]])
            for blk in range(2):
                pi = pcnt % 4; pcnt += 1
                inproj_block(hT, hB_, w_in, w_inB, 3072 + blk * 512, 512, pi)
                K.op(A, lambda e, pi=pi, blk=blk: e.activation(out=og_bf[:, blk * 512:(blk + 1) * 512], in_=ps[pi][:, :], func=AF.Sigmoid),
                     reads=[psB[pi]], writes=[ogB])
            K.dma(SP, og_s[tok, :], og_bf, reads=[ogB], writes=[allB[id(og_s)]])
            pi = pcnt % 4; pcnt += 1
            inproj_block(hT, hB_, w_in, w_inB, 4096, 16, pi)
            K.op(V, lambda e, pi=pi: e.tensor_tensor(out=zt, in0=ps[pi][:, 0:16], in1=bf_t, op=ALU.add), reads=[psB[pi], gB], writes=[ztB])
            K.op(A, lambda e: e.activation(out=zt, in_=zt, func=AF.Exp, scale=-1.0), reads=[ztB], writes=[ztB])
            K.op(V, lambda e: e.tensor_scalar(out=zt, in0=zt, scalar1=1.0, scalar2=None, op0=ALU.add), reads=[ztB], writes=[ztB])
            K.op(A, lambda e, t=t: e.activation(out=lf_tok[:, t, :], in_=zt, func=AF.Ln), reads=[ztB], writes=[lfB])
        K.phase = "m1_exchange"
        for g in range(4):
            allgather(ktF_loc[g], allB[id(ktF_loc[g])], ktF_all[g], allB[id(ktF_all[g])])
        for g in range(NVH):
            allgather(vF_loc[g], allB[id(vF_loc[g])], vF_all[g], allB[id(vF_all[g])])
        al = Bump()
        lfT = al([16, S_OWN], F32); lfTB = Buf("lfT")
        lf_rm = al([16, 2, S_OWN], F32); lf_rmB = Buf("lf_rm")
        cg = al([16, S_ALL], F32); cgB = Buf("cg")
        ones_f = al([16, S_ALL], F32); onesB = Buf("ones_f")
        hi = al([16, 2, S_OWN], BF16); mid = al([16, 2, S_OWN], BF16); lo = al([16, 2, S_OWN], BF16); splB = Buf("split")
        r1 = al([16, 2, S_OWN], F32); r1B = Buf("r1")
        co = al([16, S_OWN], F32); coB = Buf("co")
        qh = al([16, 3, S_OWN], BF16); qhB = Buf("qh")
        sel = al([128, 2], F32); selB = Buf("sel")
        K.dma(SP, sel, sel_d.partition_broadcast(128), writes=[selB])
        K.op(V, lambda e: e.tensor_scalar(out=sel, in0=sel, scalar1=-1.0, scalar2=None, op0=ALU.mult), reads=[selB], writes=[selB])
        K.op(P_, lambda e: e.memset(ones_f, 1.0), writes=[onesB])
        for t0 in range(0, NT, 4):
            pi = 4 + (t0 // 4) % 2
            for t in range(t0, min(NT, t0 + 4)):
                K.op(T, lambda e, t=t, t0=t0, pi=pi: e.transpose(ps[pi][0:16, (t - t0) * 128:(t - t0 + 1) * 128], lf_tok[:, t, :], ident_f[:, :]),
                     reads=[lfB, cB], writes=[psB[pi]])
            n_ = min(NT, t0 + 4) - t0
            K.op(V, lambda e, t0=t0, pi=pi, n_=n_: e.tensor_copy(out=lfT[:, t0 * 128:(t0 + n_) * 128], in_=ps[pi][0:16, 0:n_ * 128]),
                 reads=[psB[pi]], writes=[lfTB])
        K.dma(SP, lf_loc.ap(), lfT, reads=[lfTB], writes=[allB[id(lf_loc)]])
        allgather(lf_loc, allB[id(lf_loc)], lf_all, allB[id(lf_all)])
        K.dma(SP, lf_rm, lf_all.ap().rearrange("(r h) s -> h r s", r=2), reads=[allB[id(lf_all)]], writes=[lf_rmB])
        lf4 = lf_rm.rearrange("h r (i p) -> h r i p", p=128)
        cg3 = cg.rearrange("h (b p) -> h b p", p=128)
        hn = NT // 2

        def perm_pairs():
            return [(0, 0, 0), (0, 1, 3), (1, 0, 1), (1, 1, 2)]

        for (r, e_, gs) in perm_pairs():
            for m in range(hn):
                K.op(V, lambda e, r=r, e_=e_, gs=gs, m=m: e.tensor_copy(out=cg3[:, gs + 4 * m, :], in_=lf4[:, r, e_ + 2 * m, :]),
                     reads=[lf_rmB], writes=[cgB])
        K.op(V, lambda e: e.tensor_tensor_scan(out=cg, data0=ones_f, data1=cg, initial=0.0, op0=ALU.mult, op1=ALU.add),
             reads=[cgB, onesB], writes=[cgB])
        for (r, e_, gs) in perm_pairs():
            for m in range(hn):
                K.op(V, lambda e, r=r, e_=e_, gs=gs, m=m: e.tensor_copy(out=lf4[:, r, e_ + 2 * m, :], in_=cg3[:, gs + 4 * m, :]),
                     reads=[cgB], writes=[lf_rmB])
        K.op(V, lambda e: e.tensor_copy(out=hi, in_=lf_rm), reads=[lf_rmB], writes=[splB])
        K.op(V, lambda e: e.tensor_tensor(out=r1, in0=lf_rm, in1=hi, op=ALU.subtract), reads=[lf_rmB, splB], writes=[r1B])
        K.op(V, lambda e: e.tensor_copy(out=mid, in_=r1), reads=[r1B], writes=[splB])
        K.op(V, lambda e: e.tensor_tensor(out=r1, in0=r1, in1=mid, op=ALU.subtract), reads=[r1B, splB], writes=[r1B])
        K.op(V, lambda e: e.tensor_copy(out=lo, in_=r1), reads=[r1B], writes=[splB])
        for g in range(4):
            kv = ktF_all[g].ap().rearrange("(r h w) s -> h r w s", r=2, h=4)
            for kr in range(2):
                for ci, comp in enumerate((hi, mid, lo)):
                    K.dma(SP, kv[:, kr, 67 + ci, :], comp[g * 4:(g + 1) * 4, kr, :], reads=[splB], writes=[allB[id(ktF_all[g])]])
        K.op(V, lambda e: e.tensor_scalar(out=co, in0=lf_rm[:, 0, :], scalar1=sel[0:16, 0:1], scalar2=None, op0=ALU.mult),
             reads=[lf_rmB, selB], writes=[coB])
        K.op(V, lambda e: e.scalar_tensor_tensor(out=co, in0=lf_rm[:, 1, :], scalar=sel[0:16, 1:2], in1=co, op0=ALU.mult, op1=ALU.add),
             reads=[lf_rmB, selB, coB], writes=[coB])
        r1q = r1[:, 0, :]
        K.op(V, lambda e: e.tensor_copy(out=qh[:, 0, :], in_=co), reads=[coB], writes=[qhB])
        K.op(V, lambda e: e.tensor_tensor(out=r1q, in0=co, in1=qh[:, 0, :], op=ALU.subtract), reads=[coB, qhB], writes=[r1B])
        K.op(V, lambda e: e.tensor_copy(out=qh[:, 1, :], in_=r1q), reads=[r1B], writes=[qhB])
        K.op(V, lambda e: e.tensor_tensor(out=r1q, in0=r1q, in1=qh[:, 1, :], op=ALU.subtract), reads=[r1B, qhB], writes=[r1B])
        K.op(V, lambda e: e.tensor_copy(out=qh[:, 2, :], in_=r1q), reads=[r1B], writes=[qhB])
        K.dma(SP, qtF_s[:, 64:67, :], qh, reads=[qhB], writes=[allB[id(qtF_s)]])
        K.phase = "m1_attn"
        al = Bump()
        kt = [al([70, 2, S_OWN], BF16) for _ in range(4)]; ktBs = [Buf(f"kt{i}") for i in range(4)]
        qt = [al([70, S_OWN], BF16) for _ in range(4)]; qtBs = [Buf(f"qt{i}") for i in range(4)]
        vaug = [al([128, 2 * NT, 65], BF16) for _ in range(4)]; vBs = [Buf(f"v{i}") for i in range(4)]
        pT = [al([128, 512], BF16) for _ in range(8)]; pTB = [Buf(f"pT{i}") for i in range(8)]
        obp2 = [al([128, 4, 128], BF16) for _ in range(2)]; obp2B = [Buf("obp0"), Buf("obp1")]
        pending1 = []
        og_t = [al([128, 4, 128], BF16) for _ in range(2)]; og_tB = [Buf("og_t0"), Buf("og_t1")]
        oT = [al([128, 512], BF16) for _ in range(2)]; oTB = [Buf("oT0"), Buf("oT1")]
        for i in range(4):
            K.op(P_, lambda e, i=i: e.memset(vaug[i][:, :, 64:65], 1.0), writes=[vBs[i]])
        pcount = [0]
        ocnt = 0

        def load_pair1(hp):
            for hl in range(2):
                h = 2 * hp + hl
                b = (hp % 2) * 2 + hl
                load_kt(kt[b], ktBs[b], ktF_all, h % 4, h // 4, 70)
                K.dma(SP, qt[b], qtF_s[h], reads=[allB[id(qtF_s)]], writes=[qtBs[b]])
                load_v(vaug[b], vBs[b], vF_all, h * 64, 64)

        load_pair1(0)
        for hp in range(8):
            if hp + 1 < 8:
                load_pair1(hp + 1)
            for G in range(NG):
                oi = ocnt % 2; ocnt += 1
                obp = obp2[oi]; obpB = obp2B[oi]
                K.dma(SP, og_t[oi], og_s[G * 512:(G + 1) * 512, hp * 128:(hp + 1) * 128].rearrange("(q p) d -> p q d", p=128),
                      reads=[allB[id(og_s)]], writes=[og_tB[oi]])
                bankset = [((0, 1), (2, 3)), ((4, 5), (6, 7))]
                gens = []
                for hl in range(2):
                    b = (hp % 2) * 2 + hl
                    gens.append(attention_gen(G, kt[b], ktBs[b], qt[b], qtBs[b], vaug[b], vBs[b], 0, 70, 64, pT, pTB, pcount, *bankset[hl]))
                run_chains(gens)
                for f_ in pending1:
                    f_()
                pending1.clear()
                for hl in range(2):
                    ob_ = bankset[hl][1]
                    Ov = ov_views(ob_, 64)
                    for qb in range(4):
                        c = stcol(1)
                        K.op(V, lambda e, qb=qb, c=c, Ov=Ov: e.reciprocal(out=st[:, c:c + 1], in_=Ov[qb // 2][:, qb % 2, 64:65]),
                             reads=[psB[ob_[qb // 2]]], writes=[stB[c]])
                        K.op(V, lambda e, qb=qb, c=c, hl=hl, Ov=Ov, obp=obp: e.tensor_scalar(out=obp[:, qb, hl * 64:(hl + 1) * 64], in0=Ov[qb // 2][:, qb % 2, 0:64],
                                                                                   scalar1=st[:, c:c + 1], scalar2=None, op0=ALU.mult),
                             reads=[psB[ob_[qb // 2]], stB[c]], writes=[obpB])
                K.op(V, lambda e, oi=oi, obp=obp: e.tensor_tensor(out=obp, in0=obp, in1=og_t[oi], op=ALU.mult), reads=[obpB, og_tB[oi]], writes=[obpB])

                def out1(oi=oi, obp=obp, obpB=obpB, hp=hp, G=G):
                    transposes(obp.rearrange("p a b -> p (a b)"), obpB, 4, 128, oT[oi].rearrange("p (a b) -> p a b", a=4), oTB[oi], bank=0)
                    K.dma(SP, attn1_s[hp][:, G * 512:(G + 1) * 512], oT[oi], reads=[oTB[oi]], writes=[allB[id(attn1_s)]])
                pending1.append(out1)
        for f_ in pending1:
            f_()
        pending1.clear()
        out_proj(attn1_s, allB[id(attn1_s)], fox_w_out_d[0], 8)

    def out_proj(attn_s, attnB, w_d, nfc):
        K.phase = f"outproj{nfc}"
        al = Bump()
        w_o = [al([128, nfc, 512], BF16) for _ in range(2)]; w_oB = [Buf("wo0"), Buf("wo1")]
        aT = [al([128, nfc, 512], BF16) for _ in range(2)]; aTB = [Buf("aT0"), Buf("aT1")]
        tmp = [al([128, 512], F32) for _ in range(2)]; tmpB = [Buf("tmpo0"), Buf("tmpo1")]
        for dh in range(2):
            K.dma(P_, w_o[dh], w_d.rearrange("(k p) d -> p k d", p=128)[:, :, dh * 512:(dh + 1) * 512], writes=[w_oB[dh]])
        cnt = 0
        for G in range(NG):
            a = aT[G % 2]; aB = aTB[G % 2]
            K.dma(SP, a, attn_s.ap().rearrange("f p s -> p f s")[:, :, G * 512:(G + 1) * 512], reads=[attnB], writes=[aB])
            for c in range(4):
                t = G * 4 + c
                for dh in range(2):
                    pi = 4 + cnt % 2; tb = cnt % 2; cnt += 1
                    for fc in range(nfc):
                        K.op(T, lambda e, pi=pi, fc=fc, c=c, dh=dh, a=a: e.matmul(ps[pi][:, :], lhsT=a[:, fc, c * 128:(c + 1) * 128],
                                                                                  rhs=w_o[dh][:, fc, :], start=(fc == 0), stop=(fc == nfc - 1)),
                             reads=[aB, w_oB[dh]], writes=[psB[pi]])
                    K.op(V, lambda e, pi=pi, tb=tb, dh=dh: e.tensor_tensor(out=tmp[tb], in0=ps[pi][:, :], in1=gate[:, dh * 512:(dh + 1) * 512],
                                                                           op=ALU.mult), reads=[psB[pi], gateB], writes=[tmpB[tb]])
                    K.op(P_, lambda e, t=t, tb=tb, dh=dh: e.tensor_tensor(out=x_sb[:, t, dh * 512:(dh + 1) * 512],
                                                                          in0=x_sb[:, t, dh * 512:(dh + 1) * 512], in1=tmp[tb], op=ALU.add),
                         reads=[tmpB[tb], xB[t]], writes=[xB[t]])

    def finish():
        for t in range(NT):
            K.dma(SP, y_d[t * 128:(t + 1) * 128, :], x_sb[:, t, :], reads=[xB[t]], is_out=True)

    if "ffn1" not in skip:
        ffn(0, 0, ffw["ff1_gate"], ffw["ff1_up"], ffw["ff1_down"])
    if stop == "ffn1":
        finish()
        return nc, K
    mixer0()
    if stop in ("mix0", "m0_inproj", "m0_cc", "m0_attn"):
        finish()
        return nc, K
    ffn(0, 2, ffw["ff2_gate"], ffw["ff2_up"], ffw["ff2_down"])
    if stop == "l0":
        finish()
        return nc, K
    ffn(1, 0, ffw["ff1_gate"], ffw["ff1_up"], ffw["ff1_down"])
    if stop == "ffn3":
        finish()
        return nc, K
    mixer1()
    if stop == "mix1":
        finish()
        return nc, K
    ffn(1, 2, ffw["ff2_gate"], ffw["ff2_up"], ffw["ff2_down"])
    finish()
    return nc, K


SCOPES = [False]


def emit_program(nc, K):
    for e in ENGS:
        for op in K.ops[e]:
            for d in op.deps:
                if d.kind == "c":
                    d.signal = True
    for e in ENGS:
        n = 0
        for op in K.ops[e]:
            if op.kind == "c" and op.signal:
                n += 1
                op.val = n
    from contextlib import ExitStack
    with ExitStack() as es:
        csem = {e: es.enter_context(nc.semaphore("c_" + e)) for e in ENGS}
        dsem = {e: [es.enter_context(nc.semaphore(f"d_{e}{i}")) for i in range(NS)] for e in ("sp", "pool")}
        ccsem = [es.enter_context(nc.semaphore(f"cc{i}")) for i in range(NS)]
        block = es.enter_context(nc.Block())

        def semof(d):
            if d.kind == "c":
                return csem[d.eng], d.val
            if d.kind == "d":
                return dsem[d.eng][d.semi], d.val
            return ccsem[d.semi], d.val

        def run(ename, eng):
            for op in K.ops[ename]:
                for d in op.deps:
                    s, v = semof(d)
                    eng.wait_ge(s, v)
                if SCOPES[0]:
                    with nc.named_scope(op.phase):
                        ins = op.fn(eng)
                else:
                    ins = op.fn(eng)
                if op.kind == "c":
                    if op.signal:
                        ins.then_inc(csem[ename], 1)
                elif op.kind == "d":
                    ins.then_inc(dsem[ename][op.semi], 16)
                else:
                    ins.then_inc(ccsem[op.semi])
            if ename == "sp":
                for o in K.out_ops:
                    s, v = semof(o)
                    eng.wait_ge(s, v)

        @block.tensor
        def _(e):
            run("pe", e)

        @block.scalar
        def _(e):
            run("act", e)

        @block.vector
        def _(e):
            run("dve", e)

        @block.gpsimd
        def _(e):
            run("pool", e)

        @block.sync
        def _(e):
            run("sp", e)
    return nc


def own_blocks(r, NT):
    return [2 * i + ((i % 2) if r == 0 else 1 - (i % 2)) for i in range(NT)]


def make_in_maps(inputs, NT=16, lite=()):
    S = 2 * NT * 128
    maps = []
    ident = np.eye(128, dtype=np.float32)
    inv_a = (10000.0 ** (-np.arange(0, 64, 2, dtype=np.float32) / 64)).astype(np.float32)[None, :]
    inv_b = (10000.0 ** (-np.arange(0, 32, 2, dtype=np.float32) / 32)).astype(np.float32)[None, :]
    pp = np.arange(128)
    causal = np.where(pp[:, None] <= pp[None, :], 0.0, NEG).astype(np.float32)
    vis = np.zeros((128, 128), np.float32)
    hid = np.full((128, 128), NEG, np.float32)
    for c in range(N_CORES):
        b, r = c // 2, c % 2
        blocks = own_blocks(r, NT)
        xb = np.asarray(inputs["x"])[b, :S].reshape(2 * NT, 128, D)[blocks].reshape(NT * 128, D)
        pos = np.asarray(inputs["positions"])[b, :S].reshape(2 * NT, 128)[blocks]
        m = np.zeros((128, 4, 128), np.float32)
        for kr in range(2):
            for par in range(2):
                if kr == r:
                    mm = causal
                else:
                    own_first = (par == 0) if r == 0 else (par == 1)
                    mm = hid if own_first else vis
                m[:, kr * 2 + par, :] = mm
        d = {
            "x": np.ascontiguousarray(xb, dtype=np.float32),
            "cT": np.ascontiguousarray(np.asarray(inputs["c"])[b].reshape(8, 128).T, dtype=np.float32),
            "pos": np.ascontiguousarray(pos.T, dtype=np.int32),
            "ident_bf": ident.astype(ml_dtypes.bfloat16),
            "ident_f": ident,
            "masks": m.astype(ml_dtypes.bfloat16),
            "inv_a": inv_a, "inv_b": inv_b,
            "sel": np.array([[1.0, 0.0]] if r == 0 else [[0.0, 1.0]], np.float32),
        }
        for k in ("ada_w", "ada_b", "ff1_gate", "ff1_up", "ff1_down", "ff2_gate", "ff2_up", "ff2_down", "ab_w_in",
                  "mla_w_qb", "mla_w_kvb", "mla_q_lat_g", "mla_kv_lat_g", "diff_q_g", "diff_k_g", "mla_q_g", "mla_k_g",
                  "diff_lam_q1", "diff_lam_k1", "diff_lam_q2", "diff_lam_k2", "diff_subln_g", "ab_w_out", "fox_w_in",
                  "fox_b_f", "fox_q_g", "fox_k_g", "fox_w_out"):
            if k in lite:
                d[k] = np.zeros([1] * np.asarray(inputs[k]).ndim, np.float32)
            else:
                d[k] = np.ascontiguousarray(np.asarray(inputs[k]), dtype=np.float32)
        maps.append(d)
    return maps


def assemble(results, NT=16):
    S = 2 * NT * 128
    out = np.zeros((4, S, D), np.float32)
    for c in range(N_CORES):
        b, r = c // 2, c % 2
        blocks = own_blocks(r, NT)
        y = np.asarray(results[c]["y"]).reshape(NT, 128, D)
        ov = out[b].reshape(2 * NT, 128, D)
        for i, g in enumerate(blocks):
            ov[g] = y[i]
    return out


_CACHE = {}


def kernel(**inputs):
    if "nc" not in _CACHE:
        nc, K = build_program(16)
        emit_program(nc, K)
        _CACHE["nc"] = nc
    nc = _CACHE["nc"]
    maps = make_in_maps(inputs, 16)
    res = run_bass_kernel_spmd(nc, maps, core_ids=list(range(N_CORES)))
    return assemble(res.results, 16)
```
